# Optimizing a Trainium2 kernel written in Bass

```python
import math
import jax, jax.numpy as jnp
from jax import lax
import numpy as np

D_MODEL = 1024
BATCH = 4
SEQ = 8192
DEPTH = 1

CTX_LEN = 256
GRID_W = 64
MIX_WIDTH = D_MODEL
FOURIER_WIDTH = MIX_WIDTH // 2
N_FOURIER_GROUPS = 4
FOURIER_GROUP_DIM = FOURIER_WIDTH // N_FOURIER_GROUPS
ATTN_WIDTH = MIX_WIDTH - FOURIER_WIDTH
HEAD_DIM = 64
N_HEADS = ATTN_WIDTH // HEAD_DIM
N_KV_HEADS = 2
Q_PER_KV = N_HEADS // N_KV_HEADS
KV_WIDTH = N_KV_HEADS * HEAD_DIM
IN_WIDTH = FOURIER_WIDTH + ATTN_WIDTH + 2 * KV_WIDTH
D_FF = ((math.ceil(8 * D_MODEL / 3) + 255) // 256) * 256
ROPE_THETA = 10000.0
Q_BLOCK = 128
EPS = 1e-6

kernel_name = "hybrid_fourier_gqa_dit_block"


def rmsnorm(x, g):
    xf = x.astype(jnp.float32)
    y = xf * lax.rsqrt(jnp.mean(xf * xf, axis=-1, keepdims=True) + EPS)
    return (y * g.astype(jnp.float32)).astype(x.dtype)


def adaln(cond, w_ada, b_ada):
    return jnp.split(jax.nn.silu(cond) @ w_ada + b_ada, 6, axis=-1)


def modulate(h, shift, scale):
    return h * (1 + scale) + shift


def axial_angles(n_tokens):
    rows = n_tokens // GRID_W
    row_ids = jnp.repeat(jnp.arange(rows), GRID_W, total_repeat_length=n_tokens)
    col_ids = jnp.tile(jnp.arange(GRID_W), rows)
    n_freq = HEAD_DIM // 4
    inv_freq = ROPE_THETA ** (-jnp.arange(n_freq, dtype=jnp.float32) / n_freq)
    row_ang = row_ids.astype(jnp.float32)[:, None] * inv_freq
    col_ang = col_ids.astype(jnp.float32)[:, None] * inv_freq
    return row_ang, col_ang


def rotate(xp, ang):
    x1, x2 = jnp.split(xp, 2, axis=-1)
    cos = jnp.cos(ang).astype(xp.dtype)
    sin = jnp.sin(ang).astype(xp.dtype)
    return jnp.concatenate([x1 * cos - x2 * sin, x2 * cos + x1 * sin], axis=-1)


def apply_axial_rope(x, row_ang, col_ang):
    xr, xc = jnp.split(x, 2, axis=-1)
    return jnp.concatenate([rotate(xr, row_ang[:, None, :]),
                            rotate(xc, col_ang[:, None, :])], axis=-1)


def split_groups(p):
    b, l, _ = p.shape
    o1 = FOURIER_WIDTH
    o2 = o1 + ATTN_WIDTH
    o3 = o2 + KV_WIDTH
    u = p[..., :o1]
    q = p[..., o1:o2].reshape(b, l, N_HEADS, HEAD_DIM)
    k = p[..., o2:o3].reshape(b, l, N_KV_HEADS, HEAD_DIM)
    v = p[..., o3:].reshape(b, l, N_KV_HEADS, HEAD_DIM)
    return u, q, k, v


def fourier_mix(u, w_four):
    b, l, _ = u.shape
    ug = u.reshape(b, l, N_FOURIER_GROUPS, FOURIER_GROUP_DIM).astype(jnp.float32)
    y = jnp.fft.fft2(ug, axes=(1, 3), norm="ortho").real.astype(u.dtype)
    y = jnp.einsum('blgc,gcd->blgd', y, w_four)
    return y.reshape(b, l, FOURIER_WIDTH)


def attend_block(qb, k, v):
    s = jnp.einsum('bkgqd,bknd->bkgqn', qb, k) * (HEAD_DIM ** -0.5)
    p = jax.nn.softmax(s.astype(jnp.float32), axis=-1).astype(v.dtype)
    return jnp.einsum('bkgqn,bknd->bkgqd', p, v)


def group_queries(q):
    b, l, _, _ = q.shape
    return q.reshape(b, l, N_KV_HEADS, Q_PER_KV, HEAD_DIM).transpose(0, 2, 3, 1, 4)


def ungroup(o):
    b, _, _, l, _ = o.shape
    return o.transpose(0, 3, 1, 2, 4).reshape(b, l, ATTN_WIDTH)


def latent_attention(q, k_all, v_all):
    b, s = q.shape[0], q.shape[1]
    n_blk = s // Q_BLOCK
    qg = group_queries(q).reshape(b, N_KV_HEADS, Q_PER_KV, n_blk, Q_BLOCK, HEAD_DIM)
    qg = jnp.moveaxis(qg, 3, 0)
    out = lax.map(lambda qb: attend_block(qb, k_all, v_all), qg)
    out = jnp.moveaxis(out, 0, 3).reshape(b, N_KV_HEADS, Q_PER_KV, s, HEAD_DIM)
    return ungroup(out)


def swiglu(h, w_gate, w_up, w_down):
    return (jax.nn.silu(h @ w_gate) * (h @ w_up)) @ w_down


def setup_inputs(seed: int = 0) -> dict:
    key = jax.random.key(seed)
    ks = jax.random.split(key, 20)
    f32 = jnp.float32
    nrm = lambda k, shape, s: jax.random.normal(k, shape, f32) * s
    return {
        "x": nrm(ks[0], (BATCH, SEQ, D_MODEL), 1.0),
        "c": nrm(ks[1], (BATCH, D_MODEL), 1.0),
        "ctx": nrm(ks[2], (BATCH, CTX_LEN, D_MODEL), 1.0),
        "c_ctx": nrm(ks[3], (D_MODEL,), 1.0),
        "w_ada": nrm(ks[4], (DEPTH, D_MODEL, 6 * D_MODEL), 0.5 * D_MODEL ** -0.5),
        "b_ada": nrm(ks[5], (DEPTH, 6 * D_MODEL), 0.01),
        "g_mix": 1.0 + nrm(ks[6], (DEPTH, D_MODEL), 0.05),
        "w_in": nrm(ks[7], (DEPTH, D_MODEL, IN_WIDTH), D_MODEL ** -0.5),
        "w_four": nrm(ks[8], (DEPTH, N_FOURIER_GROUPS, FOURIER_GROUP_DIM, FOURIER_GROUP_DIM),
                      FOURIER_GROUP_DIM ** -0.5),
        "q_gain": 1.0 + nrm(ks[9], (DEPTH, HEAD_DIM), 0.05),
        "k_gain": 1.0 + nrm(ks[10], (DEPTH, HEAD_DIM), 0.05),
        "w_out": nrm(ks[11], (DEPTH, MIX_WIDTH, D_MODEL), MIX_WIDTH ** -0.5),
        "g_ffn": 1.0 + nrm(ks[12], (DEPTH, D_MODEL), 0.05),
        "w_gate": nrm(ks[13], (DEPTH, D_MODEL, D_FF), D_MODEL ** -0.5),
        "w_up": nrm(ks[14], (DEPTH, D_MODEL, D_FF), D_MODEL ** -0.5),
        "w_down": nrm(ks[15], (DEPTH, D_FF, D_MODEL), D_FF ** -0.5),
        "g_final": 1.0 + nrm(ks[16], (D_MODEL,), 0.05),
    }


def reference(x, c, ctx, c_ctx, w_ada, b_ada, g_mix, w_in, w_four, q_gain, k_gain,
              w_out, g_ffn, w_gate, w_up, w_down, g_final):
    n_lat = x.shape[1]
    row_ang, col_ang = axial_angles(n_lat)
    xc = ctx
    for l in range(DEPTH):
        last = l == DEPTH - 1
        sh1, sc1, gt1, sh2, sc2, gt2 = [m[:, None, :] for m in adaln(c, w_ada[l], b_ada[l])]
        csh1, csc1, cgt1, csh2, csc2, cgt2 = adaln(c_ctx, w_ada[l], b_ada[l])

        h = modulate(rmsnorm(x, g_mix[l]), sh1, sc1)
        hc = modulate(rmsnorm(xc, g_mix[l]), csh1, csc1)
        u, q, k, v = split_groups(h @ w_in[l])
        uc, qc, kc, vc = split_groups(hc @ w_in[l])

        q = apply_axial_rope(rmsnorm(q, q_gain[l]), row_ang, col_ang)
        k = apply_axial_rope(rmsnorm(k, k_gain[l]), row_ang, col_ang)
        kc = rmsnorm(kc, k_gain[l])
        k_ctx = kc.transpose(0, 2, 1, 3)
        v_ctx = vc.transpose(0, 2, 1, 3)
        k_all = jnp.concatenate([k.transpose(0, 2, 1, 3), k_ctx], axis=2)
        v_all = jnp.concatenate([v.transpose(0, 2, 1, 3), v_ctx], axis=2)

        mix = jnp.concatenate([fourier_mix(u, w_four[l]),
                               latent_attention(q, k_all, v_all)], axis=-1) @ w_out[l]
        x = x + gt1 * mix
        h2 = modulate(rmsnorm(x, g_ffn[l]), sh2, sc2)
        x = x + gt2 * swiglu(h2, w_gate[l], w_up[l], w_down[l])

        if not last:
            qc = rmsnorm(qc, q_gain[l])
            attn_c = ungroup(attend_block(group_queries(qc), k_ctx, v_ctx))
            mix_c = jnp.concatenate([fourier_mix(uc, w_four[l]), attn_c], axis=-1) @ w_out[l]
            xc = xc + cgt1 * mix_c
            hc2 = modulate(rmsnorm(xc, g_ffn[l]), csh2, csc2)
            xc = xc + cgt2 * swiglu(hc2, w_gate[l], w_up[l], w_down[l])
    return rmsnorm(x, g_final)
```

```python
import math
from contextlib import ExitStack

import numpy as np
import ml_dtypes

import concourse.bass as bass
import concourse.mybir as mybir
from concourse.bass_utils import run_bass_kernel_spmd

F32 = mybir.dt.float32
BF16 = mybir.dt.bfloat16
AF = mybir.ActivationFunctionType
ALU = mybir.AluOpType

D = 1024
S = 8192
B = 4
NCTX = 256
DFF = 2816
NJ = DFF // 128
EPS = 1e-6
NT = 64
NOWN = 32
NKT = 66
NKEY = NKT * 128
bf16_np = ml_dtypes.bfloat16


class Tracker:
    ENG = ["pe", "act", "dve", "pool", "sp"]
    EPOCH = 12000

    def __init__(self):
        self.ops = {e: [] for e in self.ENG}
        self.lastw = {}
        self.rd = {}
        self.bar = set()
        self.lastop = {}
        self.lastdma = {}

    def op(self, eng, fn, r=(), w=(), dma=None, ndma=1):
        ref = (eng, len(self.ops[eng]))
        deps = set(self.bar)
        for x in r:
            if x in self.lastw:
                deps.add(self.lastw[x])
        for x in w:
            if x in self.lastw:
                deps.add(self.lastw[x])
            deps.update(self.rd.get(x, ()))
        deps.discard(ref)
        o = dict(eng=eng, fn=fn, deps=deps, dma=dma, ndma=ndma, sig=False, ref=ref)
        self.ops[eng].append(o)
        for x in r:
            self.rd.setdefault(x, []).append(ref)
        for x in w:
            self.lastw[x] = ref
            self.rd[x] = []
        if dma is None:
            self.lastop[eng] = ref
        else:
            self.lastdma[dma] = ref
        return ref

    def barrier(self):
        self.bar = set(self.lastop.values()) | set(self.lastdma.values())

    def get(self, ref):
        return self.ops[ref[0]][ref[1]]

    def resolve(self):
        for e in self.ENG:
            for o in self.ops[e]:
                keep = set()
                for d in o["deps"]:
                    od = self.get(d)
                    if od["dma"] is None and d[0] == e == "pe":
                        continue
                    if od["dma"] is None and d[0] == e and d[1] >= o["ref"][1]:
                        continue
                    keep.add(d)
                    od["sig"] = True
                o["deps"] = keep
        semnames = []
        dmacnt = {}
        for e in self.ENG:
            cnt = 0
            for o in self.ops[e]:
                if o["dma"] is not None:
                    key = "d_" + o["dma"]
                    dmacnt[key] = dmacnt.get(key, 0) + o["ndma"]
                    o["sem"] = key
                    o["val"] = 16 * dmacnt[key]
                    o["sig"] = True
                    if key not in semnames:
                        semnames.append(key)
                elif o["sig"]:
                    ep = cnt // self.EPOCH
                    key = "c_%s_%d" % (e, ep)
                    o["sem"] = key
                    o["val"] = cnt % self.EPOCH + 1
                    cnt += 1
                    if key not in semnames:
                        semnames.append(key)
        self.dmatot = dmacnt
        for e in self.ENG:
            for o in self.ops[e]:
                ws = {}
                for d in o["deps"]:
                    od = self.get(d)
                    key = od["sem"]
                    val = od["val"]
                    if key.startswith("d_all"):
                        val = 16 * dmacnt[key]
                    ws[key] = max(ws.get(key, 0), val)
                o["waits"] = sorted(ws.items())
        return semnames

    def emit(self, eng, e, sems):
        waited = {}
        for o in self.ops[eng]:
            for (k, v) in o["waits"]:
                if waited.get(k, 0) < v:
                    e.wait_ge(sems[k], v)
                    waited[k] = v
            res = o["fn"](e)
            if o["dma"] is not None:
                lst = res if isinstance(res, (list, tuple)) else [res]
                assert len(lst) == o["ndma"], (len(lst), o["ndma"])
                for ins in lst:
                    ins.then_inc(sems[o["sem"]], 16)
            elif o["sig"]:
                ins = res[-1] if isinstance(res, (list, tuple)) else res
                ins.then_inc(sems[o["sem"]], 1)


class Arena:
    def __init__(self, ap):
        self.t = ap
        self.off = 0
        self.n = ap.shape[1]

    def alloc(self, nelem, dtype=F32, shape=None):
        n32 = nelem if dtype == F32 else (nelem + 1) // 2
        a = self.off
        self.off += n32
        assert self.off <= self.n, ("SBUF arena overflow", self.off, self.n)
        v = self.t[:, a:a + n32]
        if dtype != F32:
            v = v.bitcast(dtype)[:, :nelem]
        return v

    def mark(self):
        return self.off

    def reset(self, m):
        self.off = m


def v3(ap, b):
    return ap.rearrange("p (a b) -> p a b", b=b)


def tile_order(hf):
    own = [32 * hf + j for j in range(32)]
    oth = [32 * (1 - hf) + j for j in range(32)]
    return np.array(own + oth)


def host_consts(hf):
    t_of_j = tile_order(hf).astype(np.float64)
    c = {}
    c["ident"] = np.eye(128, dtype=np.float32).astype(bf16_np)
    bo = np.zeros((128, 128), np.float32)
    bo[:64, :64] = 1.0 / 64
    bo[64:, 64:] = 1.0 / 64
    c["blockones"] = bo.astype(bf16_np)
    perm = np.zeros((128, 128), np.float32)
    for d in range(128):
        dd = d % 32
        partner = d + 16 if dd < 16 else d - 16
        perm[partner, d] = 1.0
    c["perm"] = perm.astype(bf16_np)
    oa = np.zeros((128, 128), np.float32)
    oa[:, :64] = 1.0
    ob = np.zeros((128, 128), np.float32)
    ob[:, 64:] = 1.0
    c["onesA"] = oa.astype(bf16_np)
    c["onesB"] = ob.astype(bf16_np)
    inv_freq = 10000.0 ** (-np.arange(16, dtype=np.float32) / 16)
    p = np.arange(128, dtype=np.float32)
    cosT = np.zeros((128, NT, 128), np.float32)
    sinT = np.zeros((128, NT, 128), np.float32)
    tj = tile_order(hf).astype(np.float32)
    for d in range(128):
        dd = d % 64
        f = inv_freq[dd % 16]
        if dd < 32:
            ang = np.broadcast_to((p * f)[None, :], (NT, 128))
        else:
            ang = np.broadcast_to((tj * f)[:, None], (NT, 128))
        sgn = -1.0 if (dd % 32) < 16 else 1.0
        cosT[d] = np.cos(ang.astype(np.float32))
        sinT[d] = sgn * np.sin(ang.astype(np.float32))
    c["cosT"] = cosT.reshape(128, NT * 128)
    c["sinT"] = sinT.reshape(128, NT * 128)
    cc = np.arange(128, dtype=np.float64)
    angc = 2 * np.pi * np.outer(cc, cc) / 128
    c["CCm"] = (np.cos(angc) / 1024).astype(np.float32).astype(bf16_np)
    c["SCm"] = (-np.sin(angc) / 1024).astype(np.float32).astype(bf16_np)
    k1 = np.array([64 * plo + t_of_j[jj] for plo in range(2) for jj in range(32)])
    pp = np.arange(128, dtype=np.float64)
    a1 = 2 * np.pi * np.outer(pp, k1) / 128
    C1, S1 = np.cos(a1), np.sin(a1)
    c["CS1a"] = np.concatenate([C1, -S1], 1).astype(np.float32).astype(bf16_np)
    c["CS1b"] = np.concatenate([S1, C1], 1).astype(np.float32).astype(bf16_np)
    k2 = np.arange(64, dtype=np.float64)
    th = 2 * np.pi * (t_of_j[:, None, None] * k1[None, :, None] / 8192.0
                      + t_of_j[:, None, None] * k2[None, None, :] / 64.0)
    M2 = np.concatenate([np.cos(th), np.sin(th)], 0)
    c["M2"] = M2.reshape(128, 64 * 64).astype(np.float32).astype(bf16_np)
    return c


CONST_SHAPES = {
    "ident": ([128, 128], BF16), "blockones": ([128, 128], BF16), "perm": ([128, 128], BF16),
    "onesA": ([128, 128], BF16), "onesB": ([128, 128], BF16),
    "cosT": ([128, NT * 128], F32), "sinT": ([128, NT * 128], F32),
    "CCm": ([128, 128], BF16), "SCm": ([128, 128], BF16),
    "CS1a": ([128, 128], BF16), "CS1b": ([128, 128], BF16), "M2": ([128, 4096], BF16),
}

IN_SHAPES = {
    "x": ([NT, 128, D], F32), "ctx": ([2, 128, D], F32), "cc": ([128, 16], F32),
    "w_ada": ([D, 6 * D], F32), "badac": ([128, 48], F32), "badag": ([128, 2 * D], F32),
    "gvec": ([128, 16], F32), "gfin": ([128, D], F32), "qkg": ([128, 2], F32),
    "w_in": ([D, 1280], F32), "w_four": ([128, 4 * 128], F32), "w_out": ([D, D], F32),
    "w_gate": ([D, DFF], F32), "w_up": ([D, DFF], F32), "w_down": ([DFF, D], F32),
}


DEBUG = False


def build_program():
    nc = bass.Bass("TRN2", target_bir_lowering=False)
    dbg = {}
    dr = {}
    for k, (shp, dt_) in list(IN_SHAPES.items()) + list(CONST_SHAPES.items()):
        dr[k] = nc.dram_tensor(k, shp, dt_, kind="ExternalInput").ap()
    out_d = nc.dram_tensor("out", [NOWN, 128, D], F32, kind="ExternalOutput").ap()
    T1s = nc.dram_tensor("T1s", [NT, 128, 512], BF16, kind="Internal").ap()
    WGs = nc.dram_tensor("WGs", [NJ, 128, 1024], BF16, kind="Internal").ap()
    WUs = nc.dram_tensor("WUs", [NJ, 128, 1024], BF16, kind="Internal").ap()
    WDs = nc.dram_tensor("WDs", [NJ, 128, 1024], BF16, kind="Internal").ap()

    def dbg_out(name, ap_sb, res, ncols, dt_):
        if not DEBUG:
            return
        d = nc.dram_tensor("dbg_" + name, [128, ncols], dt_, kind="ExternalOutput").ap()
        dbg[name] = d
        step = 2048
        for c0 in range(0, ncols, step):
            c1 = min(ncols, c0 + step)
            tr.op("sp", (lambda e, c0=c0, c1=c1: e.dma_start(out=d[:, c0:c1], in_=ap_sb[:, c0:c1])),
                  r=res, w=[("dbg", name, c0)], dma="all9" + name)

    tr = Tracker()
    st = ExitStack()
    with st:
        NF = 53000
        sb = st.enter_context(nc.sbuf_tensor("arena", [128, NF], F32))
        ps_t = st.enter_context(nc.psum_tensor("psum", [128, 4096], F32))
        SMALL = 5248
        LO = 20864
        A = Arena(sb[:, 0:SMALL])
        ALo = Arena(sb[:, SMALL:SMALL + LO])
        AHi = Arena(sb[:, SMALL + LO:NF])
        PS = ps_t[:]

        def bank(b):
            return PS[:, b * 512:(b + 1) * 512]

        def bankbf(b):
            return PS[:, b * 512:(b + 1) * 512].bitcast(BF16)

        def pb(b):
            return ("ps", b)

        ident = A.alloc(128, BF16)
        blockones = A.alloc(128, BF16)
        perm = A.alloc(128, BF16)
        onesA = A.alloc(128, BF16)
        onesB = A.alloc(128, BF16)
        CCm = A.alloc(128, BF16)
        SCm = A.alloc(128, BF16)
        CS1a = A.alloc(128, BF16)
        CS1b = A.alloc(128, BF16)
        cc_f = A.alloc(16)
        scf = A.alloc(16)
        scbf = A.alloc(16, BF16)
        ones_bf = A.alloc(128, BF16)
        badac = A.alloc(48)
        gvec = A.alloc(16)
        qkg = A.alloc(2)
        modf = A.alloc(96)
        gm = A.alloc(16)
        gf = A.alloc(8)
        shbf = A.alloc(16, BF16)
        sh2bf = A.alloc(8, BF16)
        biasI = A.alloc(20)
        bvrow = A.alloc(256, BF16)
        bgu = A.alloc(44)
        AB = A.alloc(1024, BF16)
        gfin = A.alloc(1024)
        gt_bc = A.alloc(2048)
        eps_c = A.alloc(1)
        onec = A.alloc(1)
        junk = A.alloc(1024, BF16)
        ssb = [A.alloc(1) for _ in range(4)]
        tmb = [A.alloc(1) for _ in range(4)]
        rstd = [A.alloc(1) for _ in range(4)]
        kT = ALo.alloc(NKEY, BF16)
        Vp0 = ALo.alloc(NKT * 128, BF16)
        Vp1 = ALo.alloc(NKT * 128, BF16)
        qT = ALo.alloc(4 * 4096, BF16)
        wI = AHi.alloc(8 * 1280, BF16)
        wIc = AHi.alloc(8 * 256, BF16)
        mHiA = AHi.mark()

        for nm, dst in [("ident", ident), ("blockones", blockones), ("perm", perm),
                        ("onesA", onesA), ("onesB", onesB), ("CCm", CCm), ("SCm", SCm),
                        ("CS1a", CS1a), ("CS1b", CS1b), ("cc", cc_f), ("badac", badac),
                        ("gvec", gvec), ("qkg", qkg), ("gfin", gfin), ("badag", gt_bc)]:
            tr.op("sp", (lambda e, d=dst, s=dr[nm]: e.dma_start(out=d, in_=s)),
                  w=[nm], dma="all0")
        tr.op("dve", lambda e: e.memset(ones_bf, 1.0), w=["ones_bf"])
        tr.op("dve", lambda e: e.memset(eps_c, EPS), w=["eps_c"])
        tr.op("dve", lambda e: e.memset(onec, 1.0), w=["onec"])

        tr.op("pool", lambda e: e.memset(Vp0, 0.0), w=["Vp0z"])
        tr.op("pool", lambda e: e.memset(Vp1, 0.0), w=["Vp1z"])
        tr.op("act", lambda e: e.activation(out=scf, in_=cc_f, func=AF.Silu), r=["cc"], w=["scf"])
        tr.op("dve", lambda e: e.tensor_copy(out=scbf, in_=scf), r=["scf"], w=["scbf"])

        sc_rep = AHi.alloc(1024, BF16)
        for k in range(8):
            tr.op("dve", (lambda e, k=k: e.tensor_scalar(
                out=sc_rep[:, k * 128:(k + 1) * 128], in0=ones_bf, scalar1=scf[:, 2 * k:2 * k + 1],
                scalar2=None, op0=ALU.mult)), r=["scf", "ones_bf"], w=[("sc_rep", k)])
        wast = [AHi.alloc(2048) for _ in range(2)]
        wabf = [AHi.alloc(2048, BF16) for _ in range(2)]
        scb3 = v3(scbf, 2)
        wIp = AHi.alloc(8 * 1280, BF16)
        wist = [AHi.alloc(1280) for _ in range(2)]
        wI3 = v3(wI, 1280)
        wIc3 = v3(wIc, 256)
        wIp3 = v3(wIp, 1280)
        w_in_v = dr["w_in"].rearrange("(k p) n -> k p n", p=128)

        def win_prep(k):
            s = k % 2
            tr.op("sp", (lambda e, s=s, k=k: e.dma_start(out=wist[s], in_=w_in_v[k])),
                  w=[("wist", s)], dma="wist%d" % s)
            tr.op("act", (lambda e, s=s, k=k: e.activation(out=wIp3[:, k, :], in_=wist[s], func=AF.Copy)),
                  r=[("wist", s)], w=[("wIp", k)])
            tr.op("dve", (lambda e, s=s, k=k: e.tensor_scalar(
                out=wI3[:, k, :], in0=wist[s], scalar1=gm[:, k:k + 1], scalar2=None, op0=ALU.mult)),
                r=[("wist", s), ("gm", 0)], w=[("wI", k)])
            tr.op("act", (lambda e, s=s, k=k: e.activation(
                out=wIc3[:, k, :], in_=wist[s][:, 1024:1280], func=AF.Copy, scale=gm[:, 8 + k:9 + k])),
                r=[("wist", s), ("gm", 1)], w=[("wIc", k)])

        modv = v3(modf, 2)

        def mod_part(lo, hi, tag):
            for v in range(2):
                tr.op("dve", (lambda e, v=v: e.tensor_tensor(
                    out=modv[:, lo:hi, v], in0=v3(bank(0)[:, 0:96], 2)[:, lo:hi, v], in1=badac[:, lo:hi],
                    op=ALU.add)), r=["badac"], w=[pb(0), (tag, v)])

        for pi in range(24):
            s = pi % 2
            c0 = pi * 256
            src = dr["w_ada"].rearrange("(k p) n -> p k n", p=128)[:, :, c0:c0 + 256]
            tr.op("sp", (lambda e, s=s, src=src: e.dma_start(out=v3(wast[s], 256), in_=src)),
                  w=[("wast", s)], dma="wast%d" % s)
            tr.op("dve", (lambda e, s=s: e.tensor_copy(out=wabf[s], in_=wast[s])),
                  r=[("wast", s)], w=[("wabf", s)])
            wv = v3(wabf[s], 256)

            def colform(e, pi=pi, wv=wv):
                last = None
                for mm in range(2):
                    m = pi * 2 + mm
                    for k in range(8):
                        last = e.matmul(out=bank(0)[:, 2 * m:2 * m + 2],
                                        lhsT=wv[:, k, mm * 128:(mm + 1) * 128],
                                        rhs=scb3[:, k, :], start=(k == 0), stop=(k == 7))
                return last
            tr.op("pe", colform, r=[("wabf", s), "scbf"], w=[pb(0)])
            gi = {2: 0, 5: 1}.get(pi // 4)
            if gi is not None:
                cg = (pi % 4) * 256

                def rowform(e, wv=wv):
                    last = None
                    for k in range(8):
                        last = e.matmul(out=bank(1)[:, 0:256], lhsT=sc_rep[:, k * 128:(k + 1) * 128],
                                        rhs=wv[:, k, :], start=(k == 0), stop=(k == 7))
                    return last
                tr.op("pe", rowform, r=[("wabf", s)] + [("sc_rep", k) for k in range(8)], w=[pb(1)])
                dst = gt_bc[:, gi * 1024 + cg: gi * 1024 + cg + 256]
                tr.op("dve", (lambda e, dst=dst: e.tensor_tensor(out=dst, in0=bank(1)[:, 0:256],
                                                                 in1=dst, op=ALU.add)),
                      r=["badag"], w=[pb(1), ("gt_bc", gi, pi % 4)])
            if pi == 7:
                mod_part(0, 16, "modf1")
                M1 = [("modf1", 0), ("modf1", 1)]
                for v in range(2):
                    tr.op("dve", (lambda e, v=v: e.scalar_tensor_tensor(
                        out=gm[:, 8 * v:8 * v + 8], in0=modv[:, 8:16, v], scalar=1.0, in1=gvec[:, 0:8],
                        op0=ALU.add, op1=ALU.mult)), r=M1 + ["gvec"], w=[("gm", v)])
                tr.op("dve", lambda e: e.tensor_copy(out=v3(shbf, 2), in_=modv[:, 0:8, :]), r=M1, w=["shbf"])
            if pi >= 9 and pi % 2 == 1:
                win_prep((pi - 9) // 2)
        mod_part(16, 48, "modf")
        MODR = [("modf", 0), ("modf", 1), ("modf1", 0), ("modf1", 1)]
        tr.op("dve", lambda e: e.scalar_tensor_tensor(
            out=gf, in0=modv[:, 32:40, 0], scalar=1.0, in1=gvec[:, 8:16], op0=ALU.add, op1=ALU.mult),
            r=MODR + ["gvec"], w=["gf"])
        tr.op("dve", lambda e: e.tensor_copy(out=sh2bf, in_=modv[:, 24:32, 0]), r=MODR, w=["sh2bf"])

        w4st = AHi.alloc(512)
        w4bf = AHi.alloc(512, BF16)
        tr.op("sp", lambda e: e.dma_start(out=w4st, in_=dr["w_four"]), w=["w4st"], dma="all1")
        tr.op("dve", lambda e: e.tensor_copy(out=w4bf, in_=w4st), r=["w4st"], w=["w4bf"])

        def abmm(e):
            last = None
            for g in range(4):
                for ri, Cm in enumerate([CCm, SCm]):
                    o = g * 256 + ri * 128
                    last = e.matmul(out=PS[:, 1024 + o:1024 + o + 128], lhsT=Cm,
                                    rhs=w4bf[:, g * 128:(g + 1) * 128], start=True, stop=True)
            return last
        tr.op("pe", abmm, r=["w4bf", "CCm", "SCm"], w=[pb(2), pb(3)])
        tr.op("dve", lambda e: e.tensor_copy(out=AB, in_=PS[:, 1024:2048]), w=[pb(2), pb(3), "AB"])

        shb3 = v3(shbf, 2)

        def biasmm(e):
            last = None
            for m in range(10):
                for k in range(8):
                    last = e.matmul(out=bank(0)[:, 2 * m:2 * m + 2], lhsT=wIp3[:, k, m * 128:(m + 1) * 128],
                                    rhs=shb3[:, k, :], start=(k == 0), stop=(k == 7))
            for v in range(2):
                for k in range(8):
                    last = e.matmul(out=bank(1)[0:1, v * 128:(v + 1) * 128], lhsT=shb3[:, k, v:v + 1],
                                    rhs=wIp3[:, k, 1152:1280], start=(k == 0), stop=(k == 7))
            return last
        tr.op("pe", biasmm, r=[("wIp", k) for k in range(8)] + ["shbf"], w=[pb(0), pb(1)])
        tr.op("dve", lambda e: e.tensor_copy(out=biasI, in_=bank(0)[:, 0:20]), w=[pb(0), "biasI"])
        tr.op("dve", lambda e: e.tensor_copy(out=bvrow[0:1, :], in_=bank(1)[0:1, 0:256]), w=[pb(1), "bvrow"])
        bI3 = v3(biasI, 2)

        qT3 = v3(qT, 4096)
        Vp0_3 = v3(Vp0, 128)
        Vp1_3 = v3(Vp1, 128)
        tr.barrier()
        AHi.reset(mHiA)
        A_ = AHi
        NXT = 2
        xt = [A_.alloc(1024) for _ in range(NXT)]
        xs = [A_.alloc(1024, BF16) for _ in range(2)]
        xsT = [A_.alloc(8 * 512, BF16) for _ in range(2)]
        uTb = A_.alloc(4 * 512, BF16)
        wt = [A_.alloc(1024, BF16) for _ in range(2)]
        t1b = [A_.alloc(512, BF16) for _ in range(2)]
        qf = [A_.alloc(512) for _ in range(2)]
        sqb = [A_.alloc(512, BF16) for _ in range(2)]
        rs = [A_.alloc(512) for _ in range(2)]
        qg = [A_.alloc(512, BF16) for _ in range(2)]
        ta = [A_.alloc(512) for _ in range(2)]
        tb = [A_.alloc(512) for _ in range(2)]
        ropc = [A_.alloc(512) for _ in range(2)]
        rops = [A_.alloc(512) for _ in range(2)]
        pst = [A_.alloc(1024)]
        ppl = A_.alloc(1024, BF16)
        psc = [A_.alloc(1024, BF16)]
        gfb = A_.alloc(1024)
        for k in range(8):
            tr.op("dve", (lambda e, k=k: e.tensor_scalar(
                out=gfb[:, k * 128:(k + 1) * 128], in0=ones_bf, scalar1=gf[:, k:k + 1], scalar2=None,
                op0=ALU.mult)), r=["gf", "ones_bf"], w=[("gfb", k)])
        cnt = {"tile": 0, "in": 0}

        def rmsnorm_tile(src_xt, xt_res, xs_dst, xs_res, ti):
            s4 = ti % 4
            xt_res = xt_res if isinstance(xt_res, list) else [xt_res]
            tr.op("act", (lambda e: e.activation(out=junk, in_=src_xt, func=AF.Square, accum_out=ssb[s4])),
                  r=xt_res, w=["junk", ("ss", s4)])
            tr.op("act", (lambda e: e.activation(out=tmb[s4], in_=ssb[s4], func=AF.Ln, bias=eps_c, scale=1.0 / D)),
                  r=[("ss", s4), "eps_c"], w=[("tm", s4)])
            tr.op("act", (lambda e: e.activation(out=rstd[s4], in_=tmb[s4], func=AF.Exp, scale=-0.5)),
                  r=[("tm", s4)], w=[("rstd", s4)])
            tr.op("dve", (lambda e: e.tensor_scalar(out=xs_dst, in0=src_xt, scalar1=rstd[s4], scalar2=None,
                                                    op0=ALU.mult)),
                  r=xt_res + [("rstd", s4)], w=[xs_res])
            return s4

        def transpose_tile(xs_src, xs_res, dstT3, dst_res, col0, tpb):
            def tp(e):
                last = None
                for c in range(8):
                    last = e.transpose(out=bankbf(tpb)[:, c * 128:(c + 1) * 128],
                                       in_=xs_src[:, c * 128:(c + 1) * 128], identity=ident)
                return last
            tr.op("pe", tp, r=[xs_res, "ident"], w=[pb(tpb)])
            tr.op("dve", (lambda e: e.tensor_copy(out=dstT3[:, :, col0:col0 + 128],
                                                  in_=v3(bankbf(tpb), 128))),
                  w=[pb(tpb), dst_res])

        prep_units = []
        for j in range(NJ):
            prep_units.append(("g", j))
            prep_units.append(("u", j))
            prep_units.append(("d", j))
        prep_state = {"i": 0}
        PSCW = [("psc", 0, k) for k in range(8)]

        def prep_stage1(u):
            kind, j = prep_units[u]
            if kind in ("g", "u"):
                W = dr["w_gate"] if kind == "g" else dr["w_up"]
                src = W.rearrange("(k p) n -> p k n", p=128)[:, :, j * 128:(j + 1) * 128]
                tr.op("sp", (lambda e, src=src: e.dma_start(out=v3(pst[0], 128), in_=src)),
                      w=[("pst", 0)], dma="pst0")
            else:
                src = dr["w_down"][j * 128:(j + 1) * 128, :]
                tr.op("sp", (lambda e, src=src: e.dma_start(out=pst[0], in_=src)),
                      w=[("pst", 0)], dma="pst0")

        def prep_stage2(u):
            kind, j = prep_units[u]
            if kind in ("g", "u"):
                tr.op("act", (lambda e: e.activation(out=ppl, in_=pst[0], func=AF.Copy)),
                      r=[("pst", 0)], w=["ppl"])
                tr.op("pool", (lambda e: e.tensor_tensor(out=psc[0], in0=pst[0], in1=gfb, op=ALU.mult)),
                      r=[("pst", 0)] + [("gfb", k) for k in range(8)], w=PSCW)
                dst = (WGs if kind == "g" else WUs)[j]
                tr.op("pool", (lambda e, dst=dst: e.dma_start(out=dst, in_=psc[0])),
                      r=PSCW, w=[(kind + "s", j)], dma="pso0")
            else:
                tr.op("pool", (lambda e: e.tensor_tensor(out=psc[0], in0=pst[0],
                                                         in1=gt_bc[:, 1024:2048], op=ALU.mult)),
                      r=[("pst", 0)] + [("gt_bc", 1, q) for q in range(4)], w=PSCW)
                tr.op("pool", (lambda e, j=j: e.dma_start(out=WDs[j], in_=psc[0])),
                      r=PSCW, w=[("ds", j)], dma="pso0")

        def prep_stage3(u):
            kind, j = prep_units[u]
            if kind not in ("g", "u"):
                return
            col = j if kind == "g" else NJ + j

            def bmm(e):
                last = None
                for k in range(8):
                    last = e.matmul(out=bank(6)[:, 0:1], lhsT=ppl[:, k * 128:(k + 1) * 128],
                                    rhs=sh2bf[:, k:k + 1], start=(k == 0), stop=(k == 7))
                return last
            tr.op("pe", bmm, r=["ppl", "sh2bf"], w=[pb(6)])
            tr.op("dve", (lambda e: e.tensor_copy(out=bgu[:, col:col + 1], in_=bank(6)[:, 0:1])),
                  w=[pb(6), ("bgu", col)])

        def emit_prep(nunits):
            nu = len(prep_units)
            for _ in range(nunits):
                c = prep_state["i"]
                if c >= nu + 2:
                    return
                prep_state["i"] += 1
                if 0 <= c - 2 < nu:
                    prep_stage3(c - 2)
                if 0 <= c - 1 < nu:
                    prep_stage2(c - 1)
                if c < nu:
                    prep_stage1(c)

        WB = [6, 1]
        NDUM = 4

        def dummy(n=NDUM):
            def f(e):
                last = None
                for _ in range(n):
                    last = e.matmul(out=bank(7), lhsT=ident, rhs=wI3[:, 0, 0:512], start=True, stop=True)
                return last
            tr.op("pe", f, r=[("wI", 0), "ident"], w=[("dummybank",)])

        def chunk_s1(ck):
            st_, n, psb = ck["st"], ck["n"], 2 + ck["st"]
            xsT3, XR = ck["xsT3"], ck["XR"]
            c0w, ctxmode = ck["c0w"], ck["ctx"]

            def f(e):
                last = None
                for k in range(8):
                    lhs = wIc3[:, k, c0w:c0w + 128] if ctxmode else wI3[:, k, c0w:c0w + 128]
                    last = e.matmul(out=bank(psb)[:, :n], lhsT=lhs, rhs=xsT3[:, k, :n],
                                    start=(k == 0), stop=(k == 7))
                return last
            wres = [("wIc", k) for k in range(8)] if ctxmode else [("wI", k) for k in range(8)]
            tr.op("pe", f, r=XR + wres, w=[pb(psb)])
            tr.op("act", (lambda e: e.activation(out=qf[st_][:, :n], in_=bank(psb)[:, :n], func=AF.Identity,
                                                 bias=ck["bias"], scale=1.0)),
                  r=["biasI"], w=[pb(psb), ("qf", st_)])
            tr.op("act", (lambda e: e.activation(out=sqb[st_][:, :n], in_=qf[st_][:, :n], func=AF.Square)),
                  r=[("qf", st_)], w=[("sqb", st_)])
            dummy()
            tr.op("pe", (lambda e: e.matmul(out=bank(4 + st_)[:, :n], lhsT=blockones, rhs=sqb[st_][:, :n],
                                            start=True, stop=True)),
                  r=[("sqb", st_), "blockones"], w=[pb(4 + st_)])

        def chunk_s2(ck):
            st_, n, psb = ck["st"], ck["n"], 2 + ck["st"]
            tr.op("act", (lambda e: e.activation(out=rs[st_][:, :n], in_=bank(4 + st_)[:, :n], func=AF.Ln,
                                                 bias=eps_c, scale=1.0)),
                  r=["eps_c"], w=[pb(4 + st_), ("rs", st_)])
            tr.op("act", (lambda e: e.activation(out=rs[st_][:, :n], in_=rs[st_][:, :n], func=AF.Exp, scale=-0.5)),
                  w=[("rs", st_)])
            if ck["rope"] is None:
                tr.op("dve", (lambda e: e.scalar_tensor_tensor(out=ck["dst"], in0=qf[st_][:, :n], scalar=ck["gain"],
                                                               in1=rs[st_][:, :n], op0=ALU.mult, op1=ALU.mult)),
                      r=[("qf", st_), ("rs", st_), "qkg"], w=[ck["dres"]])
                return
            tr.op("dve", (lambda e: e.scalar_tensor_tensor(out=qg[st_][:, :n], in0=qf[st_][:, :n], scalar=ck["gain"],
                                                           in1=rs[st_][:, :n], op0=ALU.mult, op1=ALU.mult)),
                  r=[("qf", st_), ("rs", st_), "qkg"], w=[("qg", st_)])
            dummy()
            tr.op("pe", (lambda e: e.matmul(out=bank(psb)[:, :n], lhsT=perm, rhs=qg[st_][:, :n],
                                            start=True, stop=True)),
                  r=[("qg", st_), "perm"], w=[pb(psb)])

        def chunk_s3(ck):
            st_, n, psb = ck["st"], ck["n"], 2 + ck["st"]
            rsl = ck["rope"]
            if rsl is None:
                return
            tr.op("pool", (lambda e: e.tensor_tensor(out=ta[st_][:, :n], in0=qg[st_][:, :n], in1=ropc[rsl][:, :n],
                                                     op=ALU.mult)),
                  r=[("qg", st_), ("rope", rsl)], w=[("ta", st_)])
            tr.op("dve", (lambda e: e.tensor_tensor(out=tb[st_][:, :n], in0=bank(psb)[:, :n], in1=rops[rsl][:, :n],
                                                    op=ALU.mult)),
                  r=[("rope", rsl)], w=[pb(psb), ("tb", st_)])
            tr.op("pool", (lambda e: e.tensor_tensor(out=ck["dst"], in0=ta[st_][:, :n], in1=tb[st_][:, :n],
                                                     op=ALU.add)),
                  r=[("ta", st_), ("tb", st_)], w=[ck["dres"]])

        blocks = [("lat", bi) for bi in range(16)] + [("ctx", 0)]

        def blk_info(bidx):
            kind, bi = blocks[bidx]
            ntile = 4 if kind == "lat" else 2
            bs = bidx % 2
            return kind, bi, ntile, bs, v3(xsT[bs], 512), [((("xsT", bs)), i) for i in range(ntile)]

        def tiles_phase(bidx):
            kind, bi, ntile, bs, xsT3, XR = blk_info(bidx)
            if kind == "lat":
                rsl = bi % 2
                c0 = bi * 512
                tr.op("sp", (lambda e, rsl=rsl, c0=c0: [
                    e.dma_start(out=ropc[rsl], in_=dr["cosT"][:, c0:c0 + 512]),
                    e.dma_start(out=rops[rsl], in_=dr["sinT"][:, c0:c0 + 512])]),
                    w=[("rope", rsl)], dma="rope%d" % rsl, ndma=2)
            for i in range(ntile):
                ti = cnt["tile"]
                cnt["tile"] += 1
                xsl = ti % NXT
                src = dr["x"][bi * 4 + i] if kind == "lat" else dr["ctx"][i]
                tr.op("sp", (lambda e, xsl=xsl, src=src: e.dma_start(out=xt[xsl], in_=src)),
                      w=[("xt", xsl)], dma="xt%d" % xsl)
                x2 = ti % 2
                rmsnorm_tile(xt[xsl], ("xt", xsl), xs[x2], ("xs", x2), ti)
                transpose_tile(xs[x2], ("xs", x2), xsT3, XR[i], i * 128, ti % 2)

        tiles_phase(0)
        for bidx in range(len(blocks)):
            kind, bi, ntile, bs, xsT3, XR = blk_info(bidx)
            ncol = ntile * 128
            own = kind == "lat" and bi < 8
            vcol = 0 if kind == "lat" else 1
            base = dict(n=ncol, xsT3=xsT3, XR=XR, ctx=(kind == "ctx"))
            if kind == "lat":
                ck_k = dict(base, st=0, c0w=1024, bias=bI3[:, 8, 0:1], gain=qkg[:, 1:2], rope=bi % 2,
                            dst=kT[:, bi * 512:(bi + 1) * 512], dres=("kT", bi))
            else:
                ck_k = dict(base, st=0, c0w=0, bias=bI3[:, 8, 1:2], gain=qkg[:, 1:2], rope=None,
                            dst=kT[:, 8192:8192 + 256], dres=("kT", 16))
            cq = []
            if own:
                for c in range(4):
                    cq.append(dict(base, st=(c + 1) % 2, c0w=512 + c * 128, bias=bI3[:, 4 + c, 0:1],
                                   gain=qkg[:, 0:1], rope=bi % 2,
                                   dst=qT3[:, c, bi * 512:(bi + 1) * 512], dres=("qT", c, bi)))
            chunk_s1(ck_k)
            if own:
                chunk_s1(cq[0])

            def vmm(e, ncol=ncol, ntile=ntile, kind=kind, vcol=vcol, xsT3=xsT3):
                last = None
                for i in range(ntile):
                    for k in range(8):
                        rhs = wI3[:, k, 1152:1280] if kind == "lat" else wIc3[:, k, 128:256]
                        last = e.matmul(out=bank(6)[:, i * 128:(i + 1) * 128],
                                        lhsT=xsT3[:, k, i * 128:(i + 1) * 128], rhs=rhs,
                                        start=(k == 0), stop=False)
                    last = e.matmul(out=bank(6)[:, i * 128:(i + 1) * 128], lhsT=ones_bf[0:1, :],
                                    rhs=bvrow[0:1, vcol * 128:(vcol + 1) * 128], start=False, stop=True)
                return last
            tr.op("pe", vmm, r=XR + [("wI", k) for k in range(8)] + [("wIc", k) for k in range(8)]
                  + ["bvrow", "ones_bf"], w=[pb(6)])
            kt0 = bi * 4 if kind == "lat" else 64
            tr.op("dve", (lambda e, kt0=kt0, ntile=ntile: e.tensor_copy(
                out=Vp0_3[:, kt0:kt0 + ntile, 0:64], in_=v3(bank(6), 128)[:, 0:ntile, 0:64])),
                r=["Vp0z"], w=[pb(6), ("Vp0", kt0)])
            tr.op("dve", (lambda e, kt0=kt0, ntile=ntile: e.tensor_copy(
                out=Vp1_3[:, kt0:kt0 + ntile, 64:128], in_=v3(bank(6), 128)[:, 0:ntile, 64:128])),
                r=["Vp1z"], w=[pb(6), ("Vp1", kt0)])
            chunk_s2(ck_k)
            if own:
                chunk_s2(cq[0])
            if bidx + 1 < len(blocks):
                tiles_phase(bidx + 1)
            chunk_s3(ck_k)
            if own:
                chunk_s3(cq[0])
            emit_prep(1)
            if kind == "ctx":
                emit_prep(1)
                continue

            def uchunk(g, xsT3=xsT3, XR=XR):
                psb = 6 if cnt["in"] % 2 == 0 else 0
                cnt["in"] += 1

                def f(e):
                    last = None
                    for k in range(8):
                        last = e.matmul(out=bank(psb), lhsT=wI3[:, k, g * 128:(g + 1) * 128], rhs=xsT3[:, k, :],
                                        start=(k == 0), stop=(k == 7))
                    return last
                tr.op("pe", f, r=XR + [("wI", k) for k in range(8)], w=[pb(psb)])
                tr.op("act", (lambda e: e.activation(
                    out=uTb[:, g * 512:(g + 1) * 512], in_=bank(psb), func=AF.Identity,
                    bias=bI3[:, g, 0:1], scale=1.0)), r=["biasI"], w=[pb(psb), ("uTb", g)])
            if own:
                chunk_s1(cq[1])
                chunk_s1(cq[2])
                chunk_s2(cq[1])
                chunk_s2(cq[2])
                uchunk(0)
                uchunk(1)
                chunk_s3(cq[1])
                chunk_s3(cq[2])
                emit_prep(1)
                chunk_s1(cq[3])
                uchunk(2)
                chunk_s2(cq[3])
                uchunk(3)
                chunk_s3(cq[3])
            else:
                for g in range(4):
                    uchunk(g)
                    if g == 1:
                        emit_prep(1)
            for i in range(4):
                j = bi * 4 + i
                ws = j % 2
                for h in range(2):
                    def wmm(e, i=i, h=h):
                        last = None
                        for gg in range(2):
                            g = 2 * h + gg
                            last = e.matmul(out=bank(WB[h])[:, gg * 256:(gg + 1) * 256],
                                            lhsT=uTb[:, g * 512 + i * 128: g * 512 + (i + 1) * 128],
                                            rhs=AB[:, g * 256:(g + 1) * 256], start=True, stop=True)
                        return last
                    tr.op("pe", wmm, r=[("uTb", 2 * h), ("uTb", 2 * h + 1), "AB"], w=[pb(WB[h])])
                    tr.op("dve", (lambda e, ws=ws, h=h: e.tensor_copy(out=wt[ws][:, h * 512:(h + 1) * 512],
                                                                      in_=bank(WB[h]))),
                          w=[pb(WB[h]), ("wt", ws, h)])
                wt4 = wt[ws].rearrange("p (g r d) -> p g r d", g=4, r=2)
                sb_ = j % 2

                def s1mm(e, wt4=wt4, sb_=sb_):
                    e.matmul(out=bank(sb_), lhsT=CS1a, rhs=wt4[:, :, 0, :], start=True, stop=False)
                    return e.matmul(out=bank(sb_), lhsT=CS1b, rhs=wt4[:, :, 1, :], start=False, stop=True)
                tr.op("pe", s1mm, r=[("wt", ws, 0), ("wt", ws, 1), "CS1a", "CS1b"], w=[pb(sb_)])
                tr.op("act", (lambda e, ws=ws, sb_=sb_: e.activation(out=t1b[ws], in_=bank(sb_), func=AF.Copy)),
                      w=[pb(sb_), ("t1b", ws)])
                tr.op("act", (lambda e, ws=ws, j=j: e.dma_start(out=T1s[j], in_=t1b[ws])),
                      r=[("t1b", ws)], w=[("T1s", j)], dma="t1o%d" % ws)
                if i % 2 == 1:
                    emit_prep(1)
        emit_prep(100)
        dbg_out("modf", modf, [("modf", 0), ("modf", 1)], 96, F32)
        dbg_out("gt_bc", gt_bc, [("gt_bc", a, q) for a in range(2) for q in range(4)], 2048, F32)
        dbg_out("biasI", biasI, ["biasI"], 20, F32)
        dbg_out("bgu", bgu, [("bgu", q) for q in range(44)], 44, F32)
        dbg_out("kT", kT, [("kT", q) for q in range(17)], NKEY, BF16)
        dbg_out("qT", qT, [("qT", c, q) for c in range(4) for q in range(8)], 4 * 4096, BF16)
        dbg_out("Vp0", Vp0, [("Vp0", q) for q in list(range(0, 64, 4)) + [64]], NKT * 128, BF16)
        dbg_out("Vp1", Vp1, [("Vp1", q) for q in list(range(0, 64, 4)) + [64]], NKT * 128, BF16)

        tr.barrier()
        AHi.reset(0)
        attnT = AHi.alloc(4 * 4096, BF16)
        zT = AHi.alloc(4 * 4096, BF16)
        mHiC = AHi.mark()
        NPT = 6
        pT = [AHi.alloc(1024, BF16) for _ in range(NPT)]
        rd = AHi.alloc(512)
        accD = [AHi.alloc(1024) for _ in range(2)]
        tpair = [AHi.alloc(1024, BF16) for _ in range(2)]
        t3 = AHi.alloc(1024, BF16)
        ahi = AHi.alloc(1024, BF16)
        alo = AHi.alloc(1024, BF16)
        attn3 = v3(attnT, 4096)
        KR = [("kT", q) for q in range(17)]
        VR = [("Vp0", q) for q in list(range(0, 64, 4)) + [64]] + [("Vp1", q) for q in list(range(0, 64, 4)) + [64]]
        osb = AHi.alloc(512)
        un = 0
        units = [(qb, c) for qb in range(8) for c in range(4)]
        NS = 3

        def qk(u, kt):
            qb, c = units[u]
            gk = u * NKT + kt
            r3 = gk % NS
            qa = qT3[0:64, c, qb * 512:(qb + 1) * 512]
            qbb = qT3[64:128, c, qb * 512:(qb + 1) * 512]

            def f(e):
                e.matmul(out=bank(2 * r3), lhsT=kT[0:64, kt * 128:(kt + 1) * 128], rhs=qa,
                         start=True, stop=True)
                return e.matmul(out=bank(2 * r3 + 1), lhsT=kT[64:128, kt * 128:(kt + 1) * 128], rhs=qbb,
                                start=True, stop=True)
            tr.op("pe", f, r=KR + [("qT", c, qb)], w=[pb(2 * r3), pb(2 * r3 + 1)])

        def ex(u, kt):
            gk = u * NKT + kt
            r3 = gk % NS
            p4 = gk % NPT
            tr.op("act", (lambda e: e.activation(out=pT[p4], in_=PS[:, 2 * r3 * 512:(2 * r3 + 2) * 512],
                                                 func=AF.Exp, scale=0.125)),
                  w=[pb(2 * r3), pb(2 * r3 + 1), ("pT", p4)])

        def pv(u, kt):
            gk = u * NKT + kt
            p4 = gk % NPT
            u2 = u % 2

            def f(e):
                e.matmul(out=bank(6), lhsT=Vp0_3[:, kt, :], rhs=pT[p4][:, 0:512],
                         start=(kt == 0), stop=False)
                return e.matmul(out=bank(6), lhsT=Vp1_3[:, kt, :], rhs=pT[p4][:, 512:1024],
                                start=False, stop=(kt == NKT - 1))
            tr.op("pe", f, r=VR + [("pT", p4)], w=[pb(6)])
            acc = accD[u2]
            ares = ("accD", u2)
            if kt % 2 == 1:
                h2 = (kt // 2) % 2
                pa, pbb = (gk - 1) % NPT, gk % NPT
                tr.op("dve", (lambda e: e.tensor_tensor(out=tpair[h2], in0=pT[pa], in1=pT[pbb], op=ALU.add)),
                      r=[("pT", pa), ("pT", pbb)], w=[("tpair", h2)])
            if kt % 4 == 3:
                if kt == 3:
                    tr.op("dve", (lambda e: e.tensor_tensor(out=acc, in0=tpair[0], in1=tpair[1], op=ALU.add)),
                          r=[("tpair", 0), ("tpair", 1)], w=[ares])
                else:
                    tr.op("dve", (lambda e: e.tensor_tensor(out=t3, in0=tpair[0], in1=tpair[1], op=ALU.add)),
                          r=[("tpair", 0), ("tpair", 1)], w=["t3"])
                    tr.op("dve", (lambda e: e.tensor_tensor(out=acc, in0=acc, in1=t3, op=ALU.add)),
                          r=["t3"], w=[ares])
            elif kt == NKT - 1:
                tr.op("dve", (lambda e: e.tensor_tensor(out=acc, in0=acc, in1=tpair[0], op=ALU.add)),
                      r=[("tpair", 0)], w=[ares])

        def finish(u):
            qb, c = units[u]
            u2 = u % 2
            tr.op("dve", (lambda e: e.tensor_copy(out=osb, in_=bank(6))), w=[pb(6), "osb"])
            tr.op("dve", (lambda e: e.tensor_copy(out=ahi, in_=accD[u2])), r=[("accD", u2)], w=["ahi"])
            tr.op("dve", (lambda e: e.tensor_tensor(out=alo, in0=accD[u2], in1=ahi, op=ALU.subtract)),
                  r=[("accD", u2), "ahi"], w=["alo"])

        def finish2(u):
            qb, c = units[u]

            def dmm(e):
                e.matmul(out=bank(7), lhsT=onesA, rhs=ahi[:, 0:512], start=True, stop=False)
                e.matmul(out=bank(7), lhsT=onesA, rhs=alo[:, 0:512], start=False, stop=False)
                e.matmul(out=bank(7), lhsT=onesB, rhs=ahi[:, 512:1024], start=False, stop=False)
                return e.matmul(out=bank(7), lhsT=onesB, rhs=alo[:, 512:1024], start=False, stop=True)
            tr.op("pe", dmm, r=["ahi", "alo", "onesA", "onesB"], w=[pb(7)])
            tr.op("dve", (lambda e: e.reciprocal(out=rd, in_=bank(7))), w=[pb(7), "rd"])
            tr.op("dve", (lambda e: e.tensor_tensor(
                out=attn3[:, c, qb * 512:(qb + 1) * 512], in0=osb, in1=rd, op=ALU.mult)),
                r=["rd", "osb"], w=[("attnT", c, qb)])

        seq = [(u, kt) for u in range(len(units)) for kt in range(NKT)]
        AHEAD = 2
        for i in range(min(AHEAD, len(seq))):
            qk(*seq[i])
        for i, (u, kt) in enumerate(seq):
            ex(u, kt)
            if i + AHEAD < len(seq):
                qk(*seq[i + AHEAD])
            pv(u, kt)
            if kt == NKT - 1:
                finish(u)
                if u == len(units) - 1:
                    finish2(u)
            if kt == 6 and u > 0:
                finish2(u - 1)
        dbg_out("attnT", attnT, [("attnT", c, q) for c in range(4) for q in range(8)], 4 * 4096, BF16)
        tr.barrier()
        ALo.reset(0)
        T2 = ALo.alloc(64 * 512, BF16)
        M2 = ALo.alloc(4096, BF16)
        T2_3 = v3(T2, 512)
        M2_3 = v3(M2, 64)
        tr.op("sp", lambda e: e.dma_start(out=M2, in_=dr["M2"]), w=["M2"], dma="all2")
        T1v = T1s.rearrange("j (r k) c -> j r (k c)", r=2)
        for hh in range(2):
            for ri in range(2):
                tr.op("sp", (lambda e, ri=ri, hh=hh: e.dma_start(
                    out=T2[ri * 64:(ri + 1) * 64, hh * 16384:(hh + 1) * 16384],
                    in_=T1v[:, ri, hh * 16384:(hh + 1) * 16384])),
                    r=[("T1s", j) for j in range(NT)], w=[("T2", ri, hh)], dma="t2l%d%d" % (ri, hh))
        zT4 = zT.rearrange("p (g j q) -> p g j q", g=4, j=32)
        n2 = 0
        for kg in range(8):
            for g in range(4):
                psb = n2 % 2
                n2 += 1

                def s2mm(e, g=g, kg=kg, psb=psb):
                    last = None
                    for q in range(8):
                        kk = kg * 8 + q
                        last = e.matmul(out=bank(psb)[:, q * 64:(q + 1) * 64],
                                        lhsT=T2_3[:, kk, g * 128:(g + 1) * 128], rhs=M2_3[:, kk, :],
                                        start=True, stop=True)
                    return last
                tr.op("pe", s2mm, r=["M2", ("T2", 0, kg // 4), ("T2", 1, kg // 4)], w=[pb(psb)])
                plo = kg // 4
                jj0 = (kg % 4) * 8
                eng = "dve" if n2 % 2 == 0 else "act"
                dst = zT4[:, g, jj0:jj0 + 8, plo::2]
                src = v3(bank(psb), 64)
                if eng == "dve":
                    tr.op("dve", (lambda e, dst=dst, src=src: e.tensor_copy(out=dst, in_=src)),
                          w=[pb(psb), ("zT", g, kg)])
                else:
                    tr.op("act", (lambda e, dst=dst, src=src: e.activation(out=dst, in_=src, func=AF.Copy)),
                          w=[pb(psb), ("zT", g, kg)])
        dbg_out("zT", zT, [("zT", g, kg) for g in range(4) for kg in range(8)], 4 * 4096, BF16)
        dbg_out("T2", T2, [("T2", a, b_) for a in range(2) for b_ in range(2)], 64 * 512, BF16)
        tr.barrier()
        AHi.reset(mHiC)
        ALo.reset(0)
        wO = ALo.alloc(8 * 1024, BF16)
        wO3 = v3(wO, 1024)
        wost = [ALo.alloc(1024) for _ in range(1)]
        xtc = [ALo.alloc(1024) for _ in range(2)]
        x1s = [[ALo.alloc(1024) for _ in range(4)] for _ in range(2)]
        xsc = [ALo.alloc(1024, BF16) for _ in range(4)]
        h2T = ALo.alloc(8 * 512, BF16)
        h2T3 = v3(h2T, 512)
        ot = [ALo.alloc(1024) for _ in range(1)]
        aT = AHi.alloc(NJ * 512, BF16)
        aT3 = v3(aT, 512)
        sg = [AHi.alloc(512, BF16) for _ in range(2)]
        NWR = 3
        wgr = [AHi.alloc(1024, BF16) for _ in range(NWR)]
        wur = [AHi.alloc(1024, BF16) for _ in range(NWR)]
        NDR = 4
        wdr = [AHi.alloc(512, BF16) for _ in range(NDR)]
        w_out_v = dr["w_out"].rearrange("(k p) n -> k p n", p=128)
        for k in range(8):
            s = 0
            tr.op("sp", (lambda e, s=s, k=k: e.dma_start(out=wost[s], in_=w_out_v[k])),
                  w=[("wost", s)], dma="wost%d" % s)
            tr.op("dve", (lambda e, s=s, k=k: e.tensor_tensor(out=wO3[:, k, :], in0=wost[s],
                                                              in1=gt_bc[:, 0:1024], op=ALU.mult)),
                  r=[("wost", s)] + [("gt_bc", 0, q) for q in range(4)], w=[("wO", k)])
        WOR = [("wO", k) for k in range(8)]
        zT3 = v3(zT, 4096)
        ZR = [("zT", g, kg) for g in range(4) for kg in range(8)]
        cst = {"tcn": 0, "gun": 0, "ddn": 0}

        def prologue1(qb):
            x1 = x1s[qb % 2]
            for i in range(4):
                jj = qb * 4 + i
                col0 = jj * 128
                tcn = cst["tcn"]
                xsl = tcn % 2
                tr.op("pool", (lambda e, xsl=xsl, jj=jj: e.dma_start(out=xtc[xsl], in_=dr["x"][jj])),
                      w=[("xtc", xsl)], dma="xtc%d" % xsl)
                for hh in range(2):
                    psb = hh

                    def omm(e, col0=col0, hh=hh, psb=psb):
                        last = None
                        for k in range(8):
                            lhs = zT3[:, k, col0:col0 + 128] if k < 4 else attn3[:, k - 4, col0:col0 + 128]
                            last = e.matmul(out=bank(psb), lhsT=lhs, rhs=wO3[:, k, hh * 512:(hh + 1) * 512],
                                            start=(k == 0), stop=(k == 7))
                        return last
                    tr.op("pe", omm, r=WOR + ZR + [("attnT", c, qb) for c in range(4)], w=[pb(psb)])
                    tr.op("dve", (lambda e, i=i, hh=hh, psb=psb, xsl=xsl, x1=x1: e.tensor_tensor(
                        out=x1[i][:, hh * 512:(hh + 1) * 512], in0=bank(psb),
                        in1=xtc[xsl][:, hh * 512:(hh + 1) * 512], op=ALU.add)),
                        r=[("xtc", xsl)], w=[pb(psb), ("x1", qb % 2, i, hh)])
                cst["tcn"] += 1
                rmsnorm_tile(x1[i], [("x1", qb % 2, i, 0), ("x1", qb % 2, i, 1)], xsc[i], ("xsc", i), tcn)

        def prologue2(qb):
            x1 = x1s[qb % 2]
            for i in range(4):
                tcn = qb * 4 + i
                transpose_tile(xsc[i], ("xsc", i), h2T3, ("h2T", i), i * 128, 2 + tcn % 2)

        HR = [("h2T", i) for i in range(4)]
        AR = [("aT", j) for j in range(NJ)]

        def gateup(qb):
            for j in range(NJ):
                gun = cst["gun"]
                ws = gun % NWR
                g2 = gun % 2
                cst["gun"] += 1
                tr.op("sp", (lambda e, ws=ws, j=j: e.dma_start(out=wgr[ws], in_=WGs[j])),
                      r=[("gs", j)], w=[("wgr", ws)], dma="wgr%d" % ws)
                tr.op("sp", (lambda e, ws=ws, j=j: e.dma_start(out=wur[ws], in_=WUs[j])),
                      r=[("us", j)], w=[("wur", ws)], dma="wur%d" % ws)

                def gmm(e, ws=ws, g2=g2):
                    last = None
                    for k in range(8):
                        last = e.matmul(out=bank(4 + g2), lhsT=wgr[ws][:, k * 128:(k + 1) * 128],
                                        rhs=h2T3[:, k, :], start=(k == 0), stop=(k == 7))
                    return last

                def umm(e, ws=ws, g2=g2):
                    last = None
                    for k in range(8):
                        last = e.matmul(out=bank(6 + g2), lhsT=wur[ws][:, k * 128:(k + 1) * 128],
                                        rhs=h2T3[:, k, :], start=(k == 0), stop=(k == 7))
                    return last
                tr.op("pe", gmm, r=HR + [("wgr", ws)], w=[pb(4 + g2)])
                tr.op("pe", umm, r=HR + [("wur", ws)], w=[pb(6 + g2)])
                tr.op("act", (lambda e, g2=g2, j=j: e.activation(out=sg[g2], in_=bank(4 + g2), func=AF.Silu,
                                                                 bias=bgu[:, j:j + 1], scale=1.0)),
                      r=[("bgu", j)], w=[pb(4 + g2), ("sg", g2)])
                tr.op("dve", (lambda e, g2=g2, j=j: e.scalar_tensor_tensor(
                    out=aT3[:, j, :], in0=bank(6 + g2), scalar=bgu[:, NJ + j:NJ + j + 1], in1=sg[g2],
                    op0=ALU.add, op1=ALU.mult)), r=[("bgu", NJ + j), ("sg", g2)], w=[pb(6 + g2), ("aT", j)])

        def down(qb, hh):
            x1 = x1s[qb % 2]
            for j in range(NJ):
                ddn = cst["ddn"]
                ds_ = ddn % NDR
                cst["ddn"] += 1
                tr.op("sp", (lambda e, ds_=ds_, j=j, hh=hh: e.dma_start(
                    out=wdr[ds_], in_=WDs[j][:, hh * 512:(hh + 1) * 512])),
                    r=[("ds", j)], w=[("wdr", ds_)], dma="wdr%d" % ds_)

                def dmm(e, ds_=ds_, j=j):
                    last = None
                    for i in range(4):
                        last = e.matmul(out=bank(4 + i), lhsT=aT3[:, j, i * 128:(i + 1) * 128],
                                        rhs=wdr[ds_], start=(j == 0), stop=(j == NJ - 1))
                    return last
                tr.op("pe", dmm, r=AR + [("wdr", ds_)], w=[pb(4 + i) for i in range(4)])
            for i in range(4):
                tr.op("dve", (lambda e, i=i, hh=hh, x1=x1: e.tensor_tensor(
                    out=x1[i][:, hh * 512:(hh + 1) * 512], in0=bank(4 + i),
                    in1=x1[i][:, hh * 512:(hh + 1) * 512], op=ALU.add)),
                    w=[pb(4 + i), ("x1", qb % 2, i, hh)])

        def final(qb):
            x1 = x1s[qb % 2]
            for i in range(4):
                jj = qb * 4 + i
                os_ = 0
                s4 = jj % 4
                XRES = [("x1", qb % 2, i, 0), ("x1", qb % 2, i, 1)]
                tr.op("act", (lambda e, i=i, s4=s4, x1=x1: e.activation(out=junk, in_=x1[i], func=AF.Square,
                                                                       accum_out=ssb[s4])),
                      r=XRES, w=["junk", ("ss", s4)])
                tr.op("act", (lambda e, s4=s4: e.activation(out=tmb[s4], in_=ssb[s4], func=AF.Ln, bias=eps_c,
                                                            scale=1.0 / D)),
                      r=[("ss", s4), "eps_c"], w=[("tm", s4)])
                tr.op("act", (lambda e, s4=s4: e.activation(out=rstd[s4], in_=tmb[s4], func=AF.Exp, scale=-0.5)),
                      r=[("tm", s4)], w=[("rstd", s4)])
                tr.op("dve", (lambda e, i=i, s4=s4, os_=os_, x1=x1: e.scalar_tensor_tensor(
                    out=ot[os_], in0=x1[i], scalar=rstd[s4], in1=gfin, op0=ALU.mult, op1=ALU.mult)),
                    r=[("rstd", s4), "gfin"] + XRES, w=[("ot", os_)])
                tr.op("act", (lambda e, os_=os_, jj=jj: e.dma_start(out=out_d[jj], in_=ot[os_])),
                      r=[("ot", os_)], w=[("out", jj)], dma="ost%d" % os_)

        PIPE_C = True
        if not PIPE_C:
            for qb in range(8):
                prologue1(qb)
                prologue2(qb)
                gateup(qb)
                down(qb, 0)
                down(qb, 1)
                final(qb)
        else:
            prologue1(0)
            prologue2(0)
            for qb in range(8):
                gateup(qb)
                if qb + 1 < 8:
                    prologue1(qb + 1)
                    prologue2(qb + 1)
                down(qb, 0)
                down(qb, 1)
                final(qb)
        tr.op("sp", lambda e: None, r=[("out", jj) for jj in range(NOWN)] + [k for k in tr.lastw if isinstance(k, tuple) and k[0] == "dbg"], w=["done"])

        semnames = tr.resolve()
        sems = {n: st.enter_context(nc.semaphore(n)) for n in semnames}
        block = st.enter_context(nc.Block())

        @block.tensor
        def _(e):
            tr.emit("pe", e, sems)

        @block.scalar
        def _(e):
            tr.emit("act", e, sems)

        @block.vector
        def _(e):
            tr.emit("dve", e, sems)

        @block.gpsimd
        def _(e):
            tr.emit("pool", e, sems)

        @block.sync
        def _(e):
            tr.emit("sp", e, sems)
    return nc


_CACHE = {}


def _col8(v):
    return np.ascontiguousarray(np.asarray(v, np.float32).reshape(8, 128).T)


def make_in_maps(x, c, ctx, c_ctx, w_ada, b_ada, g_mix, w_in, w_four, q_gain, k_gain,
                 w_out, g_ffn, w_gate, w_up, w_down, g_final):
    f = lambda a: np.asarray(a, np.float32)
    x, c, ctx, c_ctx = f(x), f(c), f(ctx), f(c_ctx)
    w_ada, b_ada, g_mix, w_in = f(w_ada)[0], f(b_ada)[0], f(g_mix)[0], f(w_in)[0]
    w_four, q_gain, k_gain, w_out = f(w_four)[0], f(q_gain)[0], f(k_gain)[0], f(w_out)[0]
    g_ffn, w_gate, w_up, w_down, g_final = f(g_ffn)[0], f(w_gate)[0], f(w_up)[0], f(w_down)[0], f(g_final)
    qcols = []
    for cpair in range(4):
        qcols += list(range(512 + cpair * 64, 512 + cpair * 64 + 64))
        qcols += list(range(512 + (4 + cpair) * 64, 512 + (4 + cpair) * 64 + 64))
    cols = list(range(512)) + qcols + list(range(1024, 1280))
    w_in_r = np.ascontiguousarray(w_in[:, cols])
    rows = list(range(512))
    for cpair in range(4):
        rows += list(range(512 + cpair * 64, 512 + cpair * 64 + 64))
        rows += list(range(512 + (4 + cpair) * 64, 512 + (4 + cpair) * 64 + 64))
    w_out_r = np.ascontiguousarray(w_out[rows, :])
    w4 = np.ascontiguousarray(w_four.transpose(1, 0, 2).reshape(128, 512))
    badac = np.ascontiguousarray(b_ada.reshape(48, 128).T)
    badag = np.ascontiguousarray(np.broadcast_to(
        np.concatenate([b_ada[2048:3072], b_ada[5120:6144]])[None, :], (128, 2048)))
    gvec = np.concatenate([_col8(g_mix), _col8(g_ffn)], 1)
    gfin = np.ascontiguousarray(np.broadcast_to(g_final[None, :], (128, D)))
    qkg = np.stack([np.tile(q_gain, 2), np.tile(k_gain, 2)], 1).astype(np.float32)
    consts = {hf: host_consts(hf) for hf in range(2)}
    maps = []
    for core in range(8):
        b, hf = core // 2, core % 2
        order = tile_order(hf)
        xg = x[b].reshape(128, NT, D).transpose(1, 0, 2)[order]
        cc = np.zeros((128, 16), np.float32)
        cc[:, 0::2] = _col8(c[b])
        cc[:, 1::2] = _col8(c_ctx)
        m = {
            "x": np.ascontiguousarray(xg), "ctx": np.ascontiguousarray(ctx[b].reshape(2, 128, D)),
            "cc": cc, "w_ada": w_ada, "badac": badac, "badag": badag, "gvec": gvec, "gfin": gfin,
            "qkg": qkg, "w_in": w_in_r, "w_four": w4, "w_out": w_out_r,
            "w_gate": w_gate, "w_up": w_up, "w_down": w_down,
        }
        m.update(consts[hf])
        maps.append(m)
    return maps


def kernel(**inputs):
    if "nc" not in _CACHE:
        _CACHE["nc"] = build_program()
    nc = _CACHE["nc"]
    maps = make_in_maps(**inputs)
    res = run_bass_kernel_spmd(nc, maps, core_ids=list(range(8)))
    _CACHE["res"] = res
    out = np.zeros((B, S, D), np.float32)
    for core in range(8):
        b, hf = core // 2, core % 2
        o = np.asarray(res.results[core]["out"], np.float32)
        ov = out[b].reshape(128, NT, D)
        for jj in range(NOWN):
            ov[:, 32 * hf + jj, :] = o[jj]
    return out
```

```python
import math
from contextlib import ExitStack

import numpy as np
import ml_dtypes

import concourse.bass as bass
import concourse.mybir as mybir
from concourse.bass_utils import run_bass_kernel_spmd

F32 = mybir.dt.float32
BF16 = mybir.dt.bfloat16
AF = mybir.ActivationFunctionType
ALU = mybir.AluOpType

D = 1024
S = 8192
B = 4
NCTX = 256
DFF = 2816
NJ = DFF // 128
EPS = 1e-6
NT = 64
NOWN = 32
NKT = 66
NKEY = NKT * 128
bf16_np = ml_dtypes.bfloat16


class Tracker:
    ENG = ["pe", "act", "dve", "pool", "sp"]
    EPOCH = 12000

    def __init__(self):
        self.ops = {e: [] for e in self.ENG}
        self.lastw = {}
        self.rd = {}
        self.bar = set()
        self.lastop = {}
        self.lastdma = {}

    def op(self, eng, fn, r=(), w=(), dma=None, ndma=1):
        ref = (eng, len(self.ops[eng]))
        deps = set(self.bar)
        for x in r:
            if x in self.lastw:
                deps.add(self.lastw[x])
        for x in w:
            if x in self.lastw:
                deps.add(self.lastw[x])
            deps.update(self.rd.get(x, ()))
        deps.discard(ref)
        o = dict(eng=eng, fn=fn, deps=deps, dma=dma, ndma=ndma, sig=False, ref=ref)
        self.ops[eng].append(o)
        for x in r:
            self.rd.setdefault(x, []).append(ref)
        for x in w:
            self.lastw[x] = ref
            self.rd[x] = []
        if dma is None:
            self.lastop[eng] = ref
        else:
            self.lastdma[dma] = ref
        return ref

    def barrier(self):
        self.bar = set(self.lastop.values()) | set(self.lastdma.values())

    def get(self, ref):
        return self.ops[ref[0]][ref[1]]

    def resolve(self):
        for e in self.ENG:
            for o in self.ops[e]:
                keep = set()
                for d in o["deps"]:
                    od = self.get(d)
                    if od["dma"] is None and d[0] == e == "pe":
                        continue
                    if od["dma"] is None and d[0] == e and d[1] >= o["ref"][1]:
                        continue
                    keep.add(d)
                    od["sig"] = True
                o["deps"] = keep
        semnames = []
        dmacnt = {}
        for e in self.ENG:
            cnt = 0
            for o in self.ops[e]:
                if o["dma"] is not None:
                    key = "d_" + o["dma"]
                    dmacnt[key] = dmacnt.get(key, 0) + o["ndma"]
                    o["sem"] = key
                    o["val"] = 16 * dmacnt[key]
                    o["sig"] = True
                    if key not in semnames:
                        semnames.append(key)
                elif o["sig"]:
                    ep = cnt // self.EPOCH
                    key = "c_%s_%d" % (e, ep)
                    o["sem"] = key
                    o["val"] = cnt % self.EPOCH + 1
                    cnt += 1
                    if key not in semnames:
                        semnames.append(key)
        self.dmatot = dmacnt
        for e in self.ENG:
            for o in self.ops[e]:
                ws = {}
                for d in o["deps"]:
                    od = self.get(d)
                    key = od["sem"]
                    val = od["val"]
                    if key.startswith("d_all"):
                        val = 16 * dmacnt[key]
                    ws[key] = max(ws.get(key, 0), val)
                o["waits"] = sorted(ws.items())
        return semnames

    def emit(self, eng, e, sems):
        waited = {}
        for o in self.ops[eng]:
            for (k, v) in o["waits"]:
                if waited.get(k, 0) < v:
                    e.wait_ge(sems[k], v)
                    waited[k] = v
            res = o["fn"](e)
            if o["dma"] is not None:
                lst = res if isinstance(res, (list, tuple)) else [res]
                assert len(lst) == o["ndma"], (len(lst), o["ndma"])
                for ins in lst:
                    ins.then_inc(sems[o["sem"]], 16)
            elif o["sig"]:
                ins = res[-1] if isinstance(res, (list, tuple)) else res
                ins.then_inc(sems[o["sem"]], 1)


class Arena:
    def __init__(self, ap):
        self.t = ap
        self.off = 0
        self.n = ap.shape[1]

    def alloc(self, nelem, dtype=F32, shape=None):
        n32 = nelem if dtype == F32 else (nelem + 1) // 2
        a = self.off
        self.off += n32
        assert self.off <= self.n, ("SBUF arena overflow", self.off, self.n)
        v = self.t[:, a:a + n32]
        if dtype != F32:
            v = v.bitcast(dtype)[:, :nelem]
        return v

    def mark(self):
        return self.off

    def reset(self, m):
        self.off = m


def v3(ap, b):
    return ap.rearrange("p (a b) -> p a b", b=b)


def tile_order(hf):
    own = [32 * hf + j for j in range(32)]
    oth = [32 * (1 - hf) + j for j in range(32)]
    return np.array(own + oth)


def host_consts(hf):
    t_of_j = tile_order(hf).astype(np.float64)
    c = {}
    c["ident"] = np.eye(128, dtype=np.float32).astype(bf16_np)
    bo = np.zeros((128, 128), np.float32)
    bo[:64, :64] = 1.0 / 64
    bo[64:, 64:] = 1.0 / 64
    c["blockones"] = bo.astype(bf16_np)
    perm = np.zeros((128, 128), np.float32)
    for d in range(128):
        dd = d % 32
        partner = d + 16 if dd < 16 else d - 16
        perm[partner, d] = 1.0
    c["perm"] = perm.astype(bf16_np)
    oa = np.zeros((128, 128), np.float32)
    oa[:, :64] = 1.0
    ob = np.zeros((128, 128), np.float32)
    ob[:, 64:] = 1.0
    c["onesA"] = oa.astype(bf16_np)
    c["onesB"] = ob.astype(bf16_np)
    inv_freq = 10000.0 ** (-np.arange(16, dtype=np.float32) / 16)
    p = np.arange(128, dtype=np.float32)
    cosT = np.zeros((128, NT, 128), np.float32)
    sinT = np.zeros((128, NT, 128), np.float32)
    tj = tile_order(hf).astype(np.float32)
    for d in range(128):
        dd = d % 64
        f = inv_freq[dd % 16]
        if dd < 32:
            ang = np.broadcast_to((p * f)[None, :], (NT, 128))
        else:
            ang = np.broadcast_to((tj * f)[:, None], (NT, 128))
        sgn = -1.0 if (dd % 32) < 16 else 1.0
        cosT[d] = np.cos(ang.astype(np.float32))
        sinT[d] = sgn * np.sin(ang.astype(np.float32))
    c["cosT"] = cosT.reshape(128, NT * 128)
    c["sinT"] = sinT.reshape(128, NT * 128)
    cc = np.arange(128, dtype=np.float64)
    angc = 2 * np.pi * np.outer(cc, cc) / 128
    c["CCm"] = (np.cos(angc) / 1024).astype(np.float32).astype(bf16_np)
    c["SCm"] = (-np.sin(angc) / 1024).astype(np.float32).astype(bf16_np)
    k1 = np.array([64 * plo + t_of_j[jj] for plo in range(2) for jj in range(32)])
    pp = np.arange(128, dtype=np.float64)
    a1 = 2 * np.pi * np.outer(pp, k1) / 128
    C1, S1 = np.cos(a1), np.sin(a1)
    c["CS1a"] = np.concatenate([C1, -S1], 1).astype(np.float32).astype(bf16_np)
    c["CS1b"] = np.concatenate([S1, C1], 1).astype(np.float32).astype(bf16_np)
    k2 = np.arange(64, dtype=np.float64)
    th = 2 * np.pi * (t_of_j[:, None, None] * k1[None, :, None] / 8192.0
                      + t_of_j[:, None, None] * k2[None, None, :] / 64.0)
    M2 = np.concatenate([np.cos(th), np.sin(th)], 0)
    c["M2"] = M2.reshape(128, 64 * 64).astype(np.float32).astype(bf16_np)
    return c


CONST_SHAPES = {
    "ident": ([128, 128], BF16), "blockones": ([128, 128], BF16), "perm": ([128, 128], BF16),
    "onesA": ([128, 128], BF16), "onesB": ([128, 128], BF16),
    "cosT": ([128, NT * 128], F32), "sinT": ([128, NT * 128], F32),
    "CCm": ([128, 128], BF16), "SCm": ([128, 128], BF16),
    "CS1a": ([128, 128], BF16), "CS1b": ([128, 128], BF16), "M2": ([128, 4096], BF16),
}

IN_SHAPES = {
    "x": ([NT, 128, D], F32), "ctx": ([2, 128, D], F32), "cc": ([128, 16], F32),
    "w_ada": ([D, 6 * D], F32), "badac": ([128, 48], F32), "badag": ([128, 2 * D], F32),
    "gvec": ([128, 16], F32), "gfin": ([128, D], F32), "qkg": ([128, 2], F32),
    "w_in": ([D, 1280], F32), "w_four": ([128, 4 * 128], F32), "w_out": ([D, D], F32),
    "w_gate": ([D, DFF], F32), "w_up": ([D, DFF], F32), "w_down": ([DFF, D], F32),
}


DEBUG = False


def build_program():
    nc = bass.Bass("TRN2", target_bir_lowering=False)
    dbg = {}
    dr = {}
    for k, (shp, dt_) in list(IN_SHAPES.items()) + list(CONST_SHAPES.items()):
        dr[k] = nc.dram_tensor(k, shp, dt_, kind="ExternalInput").ap()
    out_d = nc.dram_tensor("out", [NOWN, 128, D], F32, kind="ExternalOutput").ap()
    T1s = nc.dram_tensor("T1s", [NT, 128, 512], BF16, kind="Internal").ap()
    WGs = nc.dram_tensor("WGs", [NJ, 128, 1024], BF16, kind="Internal").ap()
    WUs = nc.dram_tensor("WUs", [NJ, 128, 1024], BF16, kind="Internal").ap()
    WDs = nc.dram_tensor("WDs", [NJ, 128, 1024], BF16, kind="Internal").ap()

    def dbg_out(name, ap_sb, res, ncols, dt_):
        if not DEBUG:
            return
        d = nc.dram_tensor("dbg_" + name, [128, ncols], dt_, kind="ExternalOutput").ap()
        dbg[name] = d
        step = 2048
        for c0 in range(0, ncols, step):
            c1 = min(ncols, c0 + step)
            tr.op("sp", (lambda e, c0=c0, c1=c1: e.dma_start(out=d[:, c0:c1], in_=ap_sb[:, c0:c1])),
                  r=res, w=[("dbg", name, c0)], dma="all9" + name)

    tr = Tracker()
    st = ExitStack()
    with st:
        NF = 53000
        sb = st.enter_context(nc.sbuf_tensor("arena", [128, NF], F32))
        ps_t = st.enter_context(nc.psum_tensor("psum", [128, 4096], F32))
        SMALL = 5248
        LO = 20864
        A = Arena(sb[:, 0:SMALL])
        ALo = Arena(sb[:, SMALL:SMALL + LO])
        AHi = Arena(sb[:, SMALL + LO:NF])
        PS = ps_t[:]

        def bank(b):
            return PS[:, b * 512:(b + 1) * 512]

        def bankbf(b):
            return PS[:, b * 512:(b + 1) * 512].bitcast(BF16)

        def pb(b):
            return ("ps", b)

        ident = A.alloc(128, BF16)
        blockones = A.alloc(128, BF16)
        perm = A.alloc(128, BF16)
        onesA = A.alloc(128, BF16)
        onesB = A.alloc(128, BF16)
        CCm = A.alloc(128, BF16)
        SCm = A.alloc(128, BF16)
        CS1a = A.alloc(128, BF16)
        CS1b = A.alloc(128, BF16)
        cc_f = A.alloc(16)
        scf = A.alloc(16)
        scbf = A.alloc(16, BF16)
        ones_bf = A.alloc(128, BF16)
        badac = A.alloc(48)
        gvec = A.alloc(16)
        qkg = A.alloc(2)
        modf = A.alloc(96)
        gm = A.alloc(16)
        gf = A.alloc(8)
        shbf = A.alloc(16, BF16)
        sh2bf = A.alloc(8, BF16)
        biasI = A.alloc(20)
        bvrow = A.alloc(256, BF16)
        bgu = A.alloc(44)
        AB = A.alloc(1024, BF16)
        gfin = A.alloc(1024)
        gt_bc = A.alloc(2048)
        eps_c = A.alloc(1)
        onec = A.alloc(1)
        junk = A.alloc(1024, BF16)
        ssb = [A.alloc(1) for _ in range(4)]
        tmb = [A.alloc(1) for _ in range(4)]
        rstd = [A.alloc(1) for _ in range(4)]
        kT = ALo.alloc(NKEY, BF16)
        Vp0 = ALo.alloc(NKT * 128, BF16)
        Vp1 = ALo.alloc(NKT * 128, BF16)
        qT = ALo.alloc(4 * 4096, BF16)
        wI = AHi.alloc(8 * 1280, BF16)
        wIc = AHi.alloc(8 * 256, BF16)
        mHiA = AHi.mark()

        for nm, dst in [("ident", ident), ("blockones", blockones), ("perm", perm),
                        ("onesA", onesA), ("onesB", onesB), ("CCm", CCm), ("SCm", SCm),
                        ("CS1a", CS1a), ("CS1b", CS1b), ("cc", cc_f), ("badac", badac),
                        ("gvec", gvec), ("qkg", qkg), ("gfin", gfin), ("badag", gt_bc)]:
            tr.op("sp", (lambda e, d=dst, s=dr[nm]: e.dma_start(out=d, in_=s)),
                  w=[nm], dma="all0")
        tr.op("dve", lambda e: e.memset(ones_bf, 1.0), w=["ones_bf"])
        tr.op("dve", lambda e: e.memset(eps_c, EPS), w=["eps_c"])
        tr.op("dve", lambda e: e.memset(onec, 1.0), w=["onec"])

        tr.op("pool", lambda e: e.memset(Vp0, 0.0), w=["Vp0z"])
        tr.op("pool", lambda e: e.memset(Vp1, 0.0), w=["Vp1z"])
        tr.op("act", lambda e: e.activation(out=scf, in_=cc_f, func=AF.Silu), r=["cc"], w=["scf"])
        tr.op("dve", lambda e: e.tensor_copy(out=scbf, in_=scf), r=["scf"], w=["scbf"])

        sc_rep = AHi.alloc(1024, BF16)
        for k in range(8):
            tr.op("dve", (lambda e, k=k: e.tensor_scalar(
                out=sc_rep[:, k * 128:(k + 1) * 128], in0=ones_bf, scalar1=scf[:, 2 * k:2 * k + 1],
                scalar2=None, op0=ALU.mult)), r=["scf", "ones_bf"], w=[("sc_rep", k)])
        wast = [AHi.alloc(2048) for _ in range(2)]
        wabf = [AHi.alloc(2048, BF16) for _ in range(2)]
        scb3 = v3(scbf, 2)
        wIp = AHi.alloc(8 * 1280, BF16)
        wist = [AHi.alloc(1280) for _ in range(2)]
        wI3 = v3(wI, 1280)
        wIc3 = v3(wIc, 256)
        wIp3 = v3(wIp, 1280)
        w_in_v = dr["w_in"].rearrange("(k p) n -> k p n", p=128)

        def win_prep(k):
            s = k % 2
            tr.op("sp", (lambda e, s=s, k=k: e.dma_start(out=wist[s], in_=w_in_v[k])),
                  w=[("wist", s)], dma="wist%d" % s)
            tr.op("act", (lambda e, s=s, k=k: e.activation(out=wIp3[:, k, :], in_=wist[s], func=AF.Copy)),
                  r=[("wist", s)], w=[("wIp", k)])
            tr.op("dve", (lambda e, s=s, k=k: e.tensor_scalar(
                out=wI3[:, k, :], in0=wist[s], scalar1=gm[:, k:k + 1], scalar2=None, op0=ALU.mult)),
                r=[("wist", s), ("gm", 0)], w=[("wI", k)])
            tr.op("act", (lambda e, s=s, k=k: e.activation(
                out=wIc3[:, k, :], in_=wist[s][:, 1024:1280], func=AF.Copy, scale=gm[:, 8 + k:9 + k])),
                r=[("wist", s), ("gm", 1)], w=[("wIc", k)])

        modv = v3(modf, 2)

        def mod_part(lo, hi, tag):
            for v in range(2):
                tr.op("dve", (lambda e, v=v: e.tensor_tensor(
                    out=modv[:, lo:hi, v], in0=v3(bank(0)[:, 0:96], 2)[:, lo:hi, v], in1=badac[:, lo:hi],
                    op=ALU.add)), r=["badac"], w=[pb(0), (tag, v)])

        for pi in range(24):
            s = pi % 2
            c0 = pi * 256
            src = dr["w_ada"].rearrange("(k p) n -> p k n", p=128)[:, :, c0:c0 + 256]
            tr.op("sp", (lambda e, s=s, src=src: e.dma_start(out=v3(wast[s], 256), in_=src)),
                  w=[("wast", s)], dma="wast%d" % s)
            tr.op("dve", (lambda e, s=s: e.tensor_copy(out=wabf[s], in_=wast[s])),
                  r=[("wast", s)], w=[("wabf", s)])
            wv = v3(wabf[s], 256)

            def colform(e, pi=pi, wv=wv):
                last = None
                for mm in range(2):
                    m = pi * 2 + mm
                    for k in range(8):
                        last = e.matmul(out=bank(0)[:, 2 * m:2 * m + 2],
                                        lhsT=wv[:, k, mm * 128:(mm + 1) * 128],
                                        rhs=scb3[:, k, :], start=(k == 0), stop=(k == 7))
                return last
            tr.op("pe", colform, r=[("wabf", s), "scbf"], w=[pb(0)])
            gi = {2: 0, 5: 1}.get(pi // 4)
            if gi is not None:
                cg = (pi % 4) * 256

                def rowform(e, wv=wv):
                    last = None
                    for k in range(8):
                        last = e.matmul(out=bank(1)[:, 0:256], lhsT=sc_rep[:, k * 128:(k + 1) * 128],
                                        rhs=wv[:, k, :], start=(k == 0), stop=(k == 7))
                    return last
                tr.op("pe", rowform, r=[("wabf", s)] + [("sc_rep", k) for k in range(8)], w=[pb(1)])
                dst = gt_bc[:, gi * 1024 + cg: gi * 1024 + cg + 256]
                tr.op("dve", (lambda e, dst=dst: e.tensor_tensor(out=dst, in0=bank(1)[:, 0:256],
                                                                 in1=dst, op=ALU.add)),
                      r=["badag"], w=[pb(1), ("gt_bc", gi, pi % 4)])
            if pi == 7:
                mod_part(0, 16, "modf1")
                M1 = [("modf1", 0), ("modf1", 1)]
                for v in range(2):
                    tr.op("dve", (lambda e, v=v: e.scalar_tensor_tensor(
                        out=gm[:, 8 * v:8 * v + 8], in0=modv[:, 8:16, v], scalar=1.0, in1=gvec[:, 0:8],
                        op0=ALU.add, op1=ALU.mult)), r=M1 + ["gvec"], w=[("gm", v)])
                tr.op("dve", lambda e: e.tensor_copy(out=v3(shbf, 2), in_=modv[:, 0:8, :]), r=M1, w=["shbf"])
            if pi >= 9 and pi % 2 == 1:
                win_prep((pi - 9) // 2)
        mod_part(16, 48, "modf")
        MODR = [("modf", 0), ("modf", 1), ("modf1", 0), ("modf1", 1)]
        tr.op("dve", lambda e: e.scalar_tensor_tensor(
            out=gf, in0=modv[:, 32:40, 0], scalar=1.0, in1=gvec[:, 8:16], op0=ALU.add, op1=ALU.mult),
            r=MODR + ["gvec"], w=["gf"])
        tr.op("dve", lambda e: e.tensor_copy(out=sh2bf, in_=modv[:, 24:32, 0]), r=MODR, w=["sh2bf"])

        w4st = AHi.alloc(512)
        w4bf = AHi.alloc(512, BF16)
        tr.op("sp", lambda e: e.dma_start(out=w4st, in_=dr["w_four"]), w=["w4st"], dma="all1")
        tr.op("dve", lambda e: e.tensor_copy(out=w4bf, in_=w4st), r=["w4st"], w=["w4bf"])

        def abmm(e):
            last = None
            for g in range(4):
                for ri, Cm in enumerate([CCm, SCm]):
                    o = g * 256 + ri * 128
                    last = e.matmul(out=PS[:, 1024 + o:1024 + o + 128], lhsT=Cm,
                                    rhs=w4bf[:, g * 128:(g + 1) * 128], start=True, stop=True)
            return last
        tr.op("pe", abmm, r=["w4bf", "CCm", "SCm"], w=[pb(2), pb(3)])
        tr.op("dve", lambda e: e.tensor_copy(out=AB, in_=PS[:, 1024:2048]), w=[pb(2), pb(3), "AB"])

        shb3 = v3(shbf, 2)

        def biasmm(e):
            last = None
            for m in range(10):
                for k in range(8):
                    last = e.matmul(out=bank(0)[:, 2 * m:2 * m + 2], lhsT=wIp3[:, k, m * 128:(m + 1) * 128],
                                    rhs=shb3[:, k, :], start=(k == 0), stop=(k == 7))
            for v in range(2):
                for k in range(8):
                    last = e.matmul(out=bank(1)[0:1, v * 128:(v + 1) * 128], lhsT=shb3[:, k, v:v + 1],
                                    rhs=wIp3[:, k, 1152:1280], start=(k == 0), stop=(k == 7))
            return last
        tr.op("pe", biasmm, r=[("wIp", k) for k in range(8)] + ["shbf"], w=[pb(0), pb(1)])
        tr.op("dve", lambda e: e.tensor_copy(out=biasI, in_=bank(0)[:, 0:20]), w=[pb(0), "biasI"])
        tr.op("dve", lambda e: e.tensor_copy(out=bvrow[0:1, :], in_=bank(1)[0:1, 0:256]), w=[pb(1), "bvrow"])
        bI3 = v3(biasI, 2)

        qT3 = v3(qT, 4096)
        Vp0_3 = v3(Vp0, 128)
        Vp1_3 = v3(Vp1, 128)
        tr.barrier()
        AHi.reset(mHiA)
        A_ = AHi
        NXT = 2
        xt = [A_.alloc(1024) for _ in range(NXT)]
        xs = [A_.alloc(1024, BF16) for _ in range(2)]
        xsT = [A_.alloc(8 * 512, BF16) for _ in range(2)]
        uTb = A_.alloc(4 * 512, BF16)
        wt = [A_.alloc(1024, BF16) for _ in range(2)]
        t1b = [A_.alloc(512, BF16) for _ in range(2)]
        qf = [A_.alloc(512) for _ in range(2)]
        sqb = [A_.alloc(512, BF16) for _ in range(2)]
        rs = [A_.alloc(512) for _ in range(2)]
        qg = [A_.alloc(512, BF16) for _ in range(2)]
        ta = [A_.alloc(512) for _ in range(2)]
        tb = [A_.alloc(512) for _ in range(2)]
        ropc = [A_.alloc(512) for _ in range(2)]
        rops = [A_.alloc(512) for _ in range(2)]
        pst = [A_.alloc(1024)]
        ppl = A_.alloc(1024, BF16)
        psc = [A_.alloc(1024, BF16)]
        gfb = A_.alloc(1024)
        for k in range(8):
            tr.op("dve", (lambda e, k=k: e.tensor_scalar(
                out=gfb[:, k * 128:(k + 1) * 128], in0=ones_bf, scalar1=gf[:, k:k + 1], scalar2=None,
                op0=ALU.mult)), r=["gf", "ones_bf"], w=[("gfb", k)])
        cnt = {"tile": 0, "in": 0}

        def rmsnorm_tile(src_xt, xt_res, xs_dst, xs_res, ti):
            s4 = ti % 4
            xt_res = xt_res if isinstance(xt_res, list) else [xt_res]
            tr.op("act", (lambda e: e.activation(out=junk, in_=src_xt, func=AF.Square, accum_out=ssb[s4])),
                  r=xt_res, w=["junk", ("ss", s4)])
            tr.op("act", (lambda e: e.activation(out=tmb[s4], in_=ssb[s4], func=AF.Ln, bias=eps_c, scale=1.0 / D)),
                  r=[("ss", s4), "eps_c"], w=[("tm", s4)])
            tr.op("act", (lambda e: e.activation(out=rstd[s4], in_=tmb[s4], func=AF.Exp, scale=-0.5)),
                  r=[("tm", s4)], w=[("rstd", s4)])
            tr.op("dve", (lambda e: e.tensor_scalar(out=xs_dst, in0=src_xt, scalar1=rstd[s4], scalar2=None,
                                                    op0=ALU.mult)),
                  r=xt_res + [("rstd", s4)], w=[xs_res])
            return s4

        def transpose_tile(xs_src, xs_res, dstT3, dst_res, col0, tpb):
            def tp(e):
                last = None
                for c in range(8):
                    last = e.transpose(out=bankbf(tpb)[:, c * 128:(c + 1) * 128],
                                       in_=xs_src[:, c * 128:(c + 1) * 128], identity=ident)
                return last
            tr.op("pe", tp, r=[xs_res, "ident"], w=[pb(tpb)])
            tr.op("dve", (lambda e: e.tensor_copy(out=dstT3[:, :, col0:col0 + 128],
                                                  in_=v3(bankbf(tpb), 128))),
                  w=[pb(tpb), dst_res])

        prep_units = []
        for j in range(NJ):
            prep_units.append(("g", j))
            prep_units.append(("u", j))
            prep_units.append(("d", j))
        prep_state = {"i": 0}
        PSCW = [("psc", 0, k) for k in range(8)]

        def prep_stage1(u):
            kind, j = prep_units[u]
            if kind in ("g", "u"):
                W = dr["w_gate"] if kind == "g" else dr["w_up"]
                src = W.rearrange("(k p) n -> p k n", p=128)[:, :, j * 128:(j + 1) * 128]
                tr.op("sp", (lambda e, src=src: e.dma_start(out=v3(pst[0], 128), in_=src)),
                      w=[("pst", 0)], dma="pst0")
            else:
                src = dr["w_down"][j * 128:(j + 1) * 128, :]
                tr.op("sp", (lambda e, src=src: e.dma_start(out=pst[0], in_=src)),
                      w=[("pst", 0)], dma="pst0")

        def prep_stage2(u):
            kind, j = prep_units[u]
            if kind in ("g", "u"):
                tr.op("act", (lambda e: e.activation(out=ppl, in_=pst[0], func=AF.Copy)),
                      r=[("pst", 0)], w=["ppl"])
                tr.op("pool", (lambda e: e.tensor_tensor(out=psc[0], in0=pst[0], in1=gfb, op=ALU.mult)),
                      r=[("pst", 0)] + [("gfb", k) for k in range(8)], w=PSCW)
                dst = (WGs if kind == "g" else WUs)[j]
                tr.op("sp", (lambda e, dst=dst: e.dma_start(out=dst, in_=psc[0])),
                      r=PSCW, w=[(kind + "s", j)], dma="pso0")
            else:
                tr.op("pool", (lambda e: e.tensor_tensor(out=psc[0], in0=pst[0],
                                                         in1=gt_bc[:, 1024:2048], op=ALU.mult)),
                      r=[("pst", 0)] + [("gt_bc", 1, q) for q in range(4)], w=PSCW)
                tr.op("sp", (lambda e, j=j: e.dma_start(out=WDs[j], in_=psc[0])),
                      r=PSCW, w=[("ds", j)], dma="pso0")

        def prep_stage3(u):
            kind, j = prep_units[u]
            if kind not in ("g", "u"):
                return
            col = j if kind == "g" else NJ + j

            def bmm(e):
                last = None
                for k in range(8):
                    last = e.matmul(out=bank(6)[:, 0:1], lhsT=ppl[:, k * 128:(k + 1) * 128],
                                    rhs=sh2bf[:, k:k + 1], start=(k == 0), stop=(k == 7))
                return last
            tr.op("pe", bmm, r=["ppl", "sh2bf"], w=[pb(6)])
            tr.op("dve", (lambda e: e.tensor_copy(out=bgu[:, col:col + 1], in_=bank(6)[:, 0:1])),
                  w=[pb(6), ("bgu", col)])

        def emit_prep(nunits):
            nu = len(prep_units)
            for _ in range(nunits):
                c = prep_state["i"]
                if c >= nu + 2:
                    return
                prep_state["i"] += 1
                if 0 <= c - 2 < nu:
                    prep_stage3(c - 2)
                if 0 <= c - 1 < nu:
                    prep_stage2(c - 1)
                if c < nu:
                    prep_stage1(c)

        WB = [6, 1]
        NDUM = 4

        def dummy(n=NDUM):
            def f(e):
                last = None
                for _ in range(n):
                    last = e.matmul(out=bank(7), lhsT=ident, rhs=wI3[:, 0, 0:512], start=True, stop=True)
                return last
            tr.op("pe", f, r=[("wI", 0), "ident"], w=[("dummybank",)])

        def chunk_s1(ck):
            st_, n, psb = ck["st"], ck["n"], 2 + ck["st"]
            xsT3, XR = ck["xsT3"], ck["XR"]
            c0w, ctxmode = ck["c0w"], ck["ctx"]

            def f(e):
                last = None
                for k in range(8):
                    lhs = wIc3[:, k, c0w:c0w + 128] if ctxmode else wI3[:, k, c0w:c0w + 128]
                    last = e.matmul(out=bank(psb)[:, :n], lhsT=lhs, rhs=xsT3[:, k, :n],
                                    start=(k == 0), stop=(k == 7))
                return last
            wres = [("wIc", k) for k in range(8)] if ctxmode else [("wI", k) for k in range(8)]
            tr.op("pe", f, r=XR + wres, w=[pb(psb)])
            tr.op("act", (lambda e: e.activation(out=qf[st_][:, :n], in_=bank(psb)[:, :n], func=AF.Identity,
                                                 bias=ck["bias"], scale=1.0)),
                  r=["biasI"], w=[pb(psb), ("qf", st_)])
            tr.op("act", (lambda e: e.activation(out=sqb[st_][:, :n], in_=qf[st_][:, :n], func=AF.Square)),
                  r=[("qf", st_)], w=[("sqb", st_)])
            dummy()
            tr.op("pe", (lambda e: e.matmul(out=bank(4 + st_)[:, :n], lhsT=blockones, rhs=sqb[st_][:, :n],
                                            start=True, stop=True)),
                  r=[("sqb", st_), "blockones"], w=[pb(4 + st_)])

        def chunk_s2(ck):
            st_, n, psb = ck["st"], ck["n"], 2 + ck["st"]
            tr.op("act", (lambda e: e.activation(out=rs[st_][:, :n], in_=bank(4 + st_)[:, :n], func=AF.Ln,
                                                 bias=eps_c, scale=1.0)),
                  r=["eps_c"], w=[pb(4 + st_), ("rs", st_)])
            tr.op("act", (lambda e: e.activation(out=rs[st_][:, :n], in_=rs[st_][:, :n], func=AF.Exp, scale=-0.5)),
                  w=[("rs", st_)])
            if ck["rope"] is None:
                tr.op("dve", (lambda e: e.scalar_tensor_tensor(out=ck["dst"], in0=qf[st_][:, :n], scalar=ck["gain"],
                                                               in1=rs[st_][:, :n], op0=ALU.mult, op1=ALU.mult)),
                      r=[("qf", st_), ("rs", st_), "qkg"], w=[ck["dres"]])
                return
            tr.op("dve", (lambda e: e.scalar_tensor_tensor(out=qg[st_][:, :n], in0=qf[st_][:, :n], scalar=ck["gain"],
                                                           in1=rs[st_][:, :n], op0=ALU.mult, op1=ALU.mult)),
                  r=[("qf", st_), ("rs", st_), "qkg"], w=[("qg", st_)])
            dummy()
            tr.op("pe", (lambda e: e.matmul(out=bank(psb)[:, :n], lhsT=perm, rhs=qg[st_][:, :n],
                                            start=True, stop=True)),
                  r=[("qg", st_), "perm"], w=[pb(psb)])

        def chunk_s3(ck):
            st_, n, psb = ck["st"], ck["n"], 2 + ck["st"]
            rsl = ck["rope"]
            if rsl is None:
                return
            tr.op("pool", (lambda e: e.tensor_tensor(out=ta[st_][:, :n], in0=qg[st_][:, :n], in1=ropc[rsl][:, :n],
                                                     op=ALU.mult)),
                  r=[("qg", st_), ("rope", rsl)], w=[("ta", st_)])
            tr.op("dve", (lambda e: e.tensor_tensor(out=tb[st_][:, :n], in0=bank(psb)[:, :n], in1=rops[rsl][:, :n],
                                                    op=ALU.mult)),
                  r=[("rope", rsl)], w=[pb(psb), ("tb", st_)])
            tr.op("pool", (lambda e: e.tensor_tensor(out=ck["dst"], in0=ta[st_][:, :n], in1=tb[st_][:, :n],
                                                     op=ALU.add)),
                  r=[("ta", st_), ("tb", st_)], w=[ck["dres"]])

        blocks = [("lat", bi) for bi in range(16)] + [("ctx", 0)]

        def blk_info(bidx):
            kind, bi = blocks[bidx]
            ntile = 4 if kind == "lat" else 2
            bs = bidx % 2
            return kind, bi, ntile, bs, v3(xsT[bs], 512), [((("xsT", bs)), i) for i in range(ntile)]

        def tiles_phase(bidx):
            kind, bi, ntile, bs, xsT3, XR = blk_info(bidx)
            if kind == "lat":
                rsl = bi % 2
                c0 = bi * 512
                tr.op("sp", (lambda e, rsl=rsl, c0=c0: [
                    e.dma_start(out=ropc[rsl], in_=dr["cosT"][:, c0:c0 + 512]),
                    e.dma_start(out=rops[rsl], in_=dr["sinT"][:, c0:c0 + 512])]),
                    w=[("rope", rsl)], dma="rope%d" % rsl, ndma=2)
            for i in range(ntile):
                ti = cnt["tile"]
                cnt["tile"] += 1
                xsl = ti % NXT
                src = dr["x"][bi * 4 + i] if kind == "lat" else dr["ctx"][i]
                tr.op("sp", (lambda e, xsl=xsl, src=src: e.dma_start(out=xt[xsl], in_=src)),
                      w=[("xt", xsl)], dma="xt%d" % xsl)
                x2 = ti % 2
                rmsnorm_tile(xt[xsl], ("xt", xsl), xs[x2], ("xs", x2), ti)
                transpose_tile(xs[x2], ("xs", x2), xsT3, XR[i], i * 128, ti % 2)

        tiles_phase(0)
        for bidx in range(len(blocks)):
            kind, bi, ntile, bs, xsT3, XR = blk_info(bidx)
            ncol = ntile * 128
            own = kind == "lat" and bi < 8
            vcol = 0 if kind == "lat" else 1
            base = dict(n=ncol, xsT3=xsT3, XR=XR, ctx=(kind == "ctx"))
            if kind == "lat":
                ck_k = dict(base, st=0, c0w=1024, bias=bI3[:, 8, 0:1], gain=qkg[:, 1:2], rope=bi % 2,
                            dst=kT[:, bi * 512:(bi + 1) * 512], dres=("kT", bi))
            else:
                ck_k = dict(base, st=0, c0w=0, bias=bI3[:, 8, 1:2], gain=qkg[:, 1:2], rope=None,
                            dst=kT[:, 8192:8192 + 256], dres=("kT", 16))
            cq = []
            if own:
                for c in range(4):
                    cq.append(dict(base, st=(c + 1) % 2, c0w=512 + c * 128, bias=bI3[:, 4 + c, 0:1],
                                   gain=qkg[:, 0:1], rope=bi % 2,
                                   dst=qT3[:, c, bi * 512:(bi + 1) * 512], dres=("qT", c, bi)))
            chunk_s1(ck_k)
            if own:
                chunk_s1(cq[0])

            def vmm(e, ncol=ncol, ntile=ntile, kind=kind, vcol=vcol, xsT3=xsT3):
                last = None
                for i in range(ntile):
                    for k in range(8):
                        rhs = wI3[:, k, 1152:1280] if kind == "lat" else wIc3[:, k, 128:256]
                        last = e.matmul(out=bank(6)[:, i * 128:(i + 1) * 128],
                                        lhsT=xsT3[:, k, i * 128:(i + 1) * 128], rhs=rhs,
                                        start=(k == 0), stop=False)
                    last = e.matmul(out=bank(6)[:, i * 128:(i + 1) * 128], lhsT=ones_bf[0:1, :],
                                    rhs=bvrow[0:1, vcol * 128:(vcol + 1) * 128], start=False, stop=True)
                return last
            tr.op("pe", vmm, r=XR + [("wI", k) for k in range(8)] + [("wIc", k) for k in range(8)]
                  + ["bvrow", "ones_bf"], w=[pb(6)])
            kt0 = bi * 4 if kind == "lat" else 64
            tr.op("dve", (lambda e, kt0=kt0, ntile=ntile: e.tensor_copy(
                out=Vp0_3[:, kt0:kt0 + ntile, 0:64], in_=v3(bank(6), 128)[:, 0:ntile, 0:64])),
                r=["Vp0z"], w=[pb(6), ("Vp0", kt0)])
            tr.op("dve", (lambda e, kt0=kt0, ntile=ntile: e.tensor_copy(
                out=Vp1_3[:, kt0:kt0 + ntile, 64:128], in_=v3(bank(6), 128)[:, 0:ntile, 64:128])),
                r=["Vp1z"], w=[pb(6), ("Vp1", kt0)])
            chunk_s2(ck_k)
            if own:
                chunk_s2(cq[0])
            if bidx + 1 < len(blocks):
                tiles_phase(bidx + 1)
            chunk_s3(ck_k)
            if own:
                chunk_s3(cq[0])
            emit_prep(1)
            if kind == "ctx":
                emit_prep(1)
                continue

            def uchunk(g, xsT3=xsT3, XR=XR):
                psb = 6 if cnt["in"] % 2 == 0 else 0
                cnt["in"] += 1

                def f(e):
                    last = None
                    for k in range(8):
                        last = e.matmul(out=bank(psb), lhsT=wI3[:, k, g * 128:(g + 1) * 128], rhs=xsT3[:, k, :],
                                        start=(k == 0), stop=(k == 7))
                    return last
                tr.op("pe", f, r=XR + [("wI", k) for k in range(8)], w=[pb(psb)])
                tr.op("act", (lambda e: e.activation(
                    out=uTb[:, g * 512:(g + 1) * 512], in_=bank(psb), func=AF.Identity,
                    bias=bI3[:, g, 0:1], scale=1.0)), r=["biasI"], w=[pb(psb), ("uTb", g)])
            if own:
                chunk_s1(cq[1])
                chunk_s1(cq[2])
                chunk_s2(cq[1])
                chunk_s2(cq[2])
                uchunk(0)
                uchunk(1)
                chunk_s3(cq[1])
                chunk_s3(cq[2])
                emit_prep(1)
                chunk_s1(cq[3])
                uchunk(2)
                chunk_s2(cq[3])
                uchunk(3)
                chunk_s3(cq[3])
            else:
                for g in range(4):
                    uchunk(g)
                    if g == 1:
                        emit_prep(1)
            for i in range(4):
                j = bi * 4 + i
                ws = j % 2
                for h in range(2):
                    def wmm(e, i=i, h=h):
                        last = None
                        for gg in range(2):
                            g = 2 * h + gg
                            last = e.matmul(out=bank(WB[h])[:, gg * 256:(gg + 1) * 256],
                                            lhsT=uTb[:, g * 512 + i * 128: g * 512 + (i + 1) * 128],
                                            rhs=AB[:, g * 256:(g + 1) * 256], start=True, stop=True)
                        return last
                    tr.op("pe", wmm, r=[("uTb", 2 * h), ("uTb", 2 * h + 1), "AB"], w=[pb(WB[h])])
                    tr.op("dve", (lambda e, ws=ws, h=h: e.tensor_copy(out=wt[ws][:, h * 512:(h + 1) * 512],
                                                                      in_=bank(WB[h]))),
                          w=[pb(WB[h]), ("wt", ws, h)])
                wt4 = wt[ws].rearrange("p (g r d) -> p g r d", g=4, r=2)
                sb_ = j % 2

                def s1mm(e, wt4=wt4, sb_=sb_):
                    e.matmul(out=bank(sb_), lhsT=CS1a, rhs=wt4[:, :, 0, :], start=True, stop=False)
                    return e.matmul(out=bank(sb_), lhsT=CS1b, rhs=wt4[:, :, 1, :], start=False, stop=True)
                tr.op("pe", s1mm, r=[("wt", ws, 0), ("wt", ws, 1), "CS1a", "CS1b"], w=[pb(sb_)])
                tr.op("act", (lambda e, ws=ws, sb_=sb_: e.activation(out=t1b[ws], in_=bank(sb_), func=AF.Copy)),
                      w=[pb(sb_), ("t1b", ws)])
                tr.op("act", (lambda e, ws=ws, j=j: e.dma_start(out=T1s[j], in_=t1b[ws])),
                      r=[("t1b", ws)], w=[("T1s", j)], dma="t1o%d" % ws)
                if i % 2 == 1:
                    emit_prep(1)
        emit_prep(100)
        dbg_out("modf", modf, [("modf", 0), ("modf", 1)], 96, F32)
        dbg_out("gt_bc", gt_bc, [("gt_bc", a, q) for a in range(2) for q in range(4)], 2048, F32)
        dbg_out("biasI", biasI, ["biasI"], 20, F32)
        dbg_out("bgu", bgu, [("bgu", q) for q in range(44)], 44, F32)
        dbg_out("kT", kT, [("kT", q) for q in range(17)], NKEY, BF16)
        dbg_out("qT", qT, [("qT", c, q) for c in range(4) for q in range(8)], 4 * 4096, BF16)
        dbg_out("Vp0", Vp0, [("Vp0", q) for q in list(range(0, 64, 4)) + [64]], NKT * 128, BF16)
        dbg_out("Vp1", Vp1, [("Vp1", q) for q in list(range(0, 64, 4)) + [64]], NKT * 128, BF16)

        tr.barrier()
        AHi.reset(0)
        attnT = AHi.alloc(4 * 4096, BF16)
        zT = AHi.alloc(4 * 4096, BF16)
        mHiC = AHi.mark()
        NPT = 6
        pT = [AHi.alloc(1024, BF16) for _ in range(NPT)]
        rd = AHi.alloc(512)
        accD = [AHi.alloc(1024) for _ in range(2)]
        tpair = [AHi.alloc(1024, BF16) for _ in range(2)]
        t3 = AHi.alloc(1024, BF16)
        ahi = AHi.alloc(1024, BF16)
        alo = AHi.alloc(1024, BF16)
        attn3 = v3(attnT, 4096)
        KR = [("kT", q) for q in range(17)]
        VR = [("Vp0", q) for q in list(range(0, 64, 4)) + [64]] + [("Vp1", q) for q in list(range(0, 64, 4)) + [64]]
        osb = AHi.alloc(512)
        un = 0
        units = [(qb, c) for qb in range(8) for c in range(4)]
        NS = 3

        def qk(u, kt):
            qb, c = units[u]
            gk = u * NKT + kt
            r3 = gk % NS
            qa = qT3[0:64, c, qb * 512:(qb + 1) * 512]
            qbb = qT3[64:128, c, qb * 512:(qb + 1) * 512]

            def f(e):
                e.matmul(out=bank(2 * r3), lhsT=kT[0:64, kt * 128:(kt + 1) * 128], rhs=qa,
                         start=True, stop=True)
                return e.matmul(out=bank(2 * r3 + 1), lhsT=kT[64:128, kt * 128:(kt + 1) * 128], rhs=qbb,
                                start=True, stop=True)
            tr.op("pe", f, r=KR + [("qT", c, qb)], w=[pb(2 * r3), pb(2 * r3 + 1)])

        def ex(u, kt):
            gk = u * NKT + kt
            r3 = gk % NS
            p4 = gk % NPT
            tr.op("act", (lambda e: e.activation(out=pT[p4], in_=PS[:, 2 * r3 * 512:(2 * r3 + 2) * 512],
                                                 func=AF.Exp, scale=0.125)),
                  w=[pb(2 * r3), pb(2 * r3 + 1), ("pT", p4)])

        def pv(u, kt):
            gk = u * NKT + kt
            p4 = gk % NPT
            u2 = u % 2

            def f(e):
                e.matmul(out=bank(6), lhsT=Vp0_3[:, kt, :], rhs=pT[p4][:, 0:512],
                         start=(kt == 0), stop=False)
                return e.matmul(out=bank(6), lhsT=Vp1_3[:, kt, :], rhs=pT[p4][:, 512:1024],
                                start=False, stop=(kt == NKT - 1))
            tr.op("pe", f, r=VR + [("pT", p4)], w=[pb(6)])
            acc = accD[u2]
            ares = ("accD", u2)
            if kt % 2 == 1:
                h2 = (kt // 2) % 2
                pa, pbb = (gk - 1) % NPT, gk % NPT
                tr.op("dve", (lambda e: e.tensor_tensor(out=tpair[h2], in0=pT[pa], in1=pT[pbb], op=ALU.add)),
                      r=[("pT", pa), ("pT", pbb)], w=[("tpair", h2)])
            if kt % 4 == 3:
                if kt == 3:
                    tr.op("dve", (lambda e: e.tensor_tensor(out=acc, in0=tpair[0], in1=tpair[1], op=ALU.add)),
                          r=[("tpair", 0), ("tpair", 1)], w=[ares])
                else:
                    tr.op("dve", (lambda e: e.tensor_tensor(out=t3, in0=tpair[0], in1=tpair[1], op=ALU.add)),
                          r=[("tpair", 0), ("tpair", 1)], w=["t3"])
                    tr.op("dve", (lambda e: e.tensor_tensor(out=acc, in0=acc, in1=t3, op=ALU.add)),
                          r=["t3"], w=[ares])
            elif kt == NKT - 1:
                tr.op("dve", (lambda e: e.tensor_tensor(out=acc, in0=acc, in1=tpair[0], op=ALU.add)),
                      r=[("tpair", 0)], w=[ares])

        def finish(u):
            qb, c = units[u]
            u2 = u % 2
            tr.op("dve", (lambda e: e.tensor_copy(out=osb, in_=bank(6))), w=[pb(6), "osb"])
            tr.op("dve", (lambda e: e.tensor_copy(out=ahi, in_=accD[u2])), r=[("accD", u2)], w=["ahi"])
            tr.op("dve", (lambda e: e.tensor_tensor(out=alo, in0=accD[u2], in1=ahi, op=ALU.subtract)),
                  r=[("accD", u2), "ahi"], w=["alo"])

        def finish2(u):
            qb, c = units[u]

            def dmm(e):
                e.matmul(out=bank(7), lhsT=onesA, rhs=ahi[:, 0:512], start=True, stop=False)
                e.matmul(out=bank(7), lhsT=onesA, rhs=alo[:, 0:512], start=False, stop=False)
                e.matmul(out=bank(7), lhsT=onesB, rhs=ahi[:, 512:1024], start=False, stop=False)
                return e.matmul(out=bank(7), lhsT=onesB, rhs=alo[:, 512:1024], start=False, stop=True)
            tr.op("pe", dmm, r=["ahi", "alo", "onesA", "onesB"], w=[pb(7)])
            tr.op("dve", (lambda e: e.reciprocal(out=rd, in_=bank(7))), w=[pb(7), "rd"])
            tr.op("dve", (lambda e: e.tensor_tensor(
                out=attn3[:, c, qb * 512:(qb + 1) * 512], in0=osb, in1=rd, op=ALU.mult)),
                r=["rd", "osb"], w=[("attnT", c, qb)])

        seq = [(u, kt) for u in range(len(units)) for kt in range(NKT)]
        AHEAD = 2
        for i in range(min(AHEAD, len(seq))):
            qk(*seq[i])
        for i, (u, kt) in enumerate(seq):
            ex(u, kt)
            if i + AHEAD < len(seq):
                qk(*seq[i + AHEAD])
            pv(u, kt)
            if kt == NKT - 1:
                finish(u)
                if u == len(units) - 1:
                    finish2(u)
            if kt == 6 and u > 0:
                finish2(u - 1)
        dbg_out("attnT", attnT, [("attnT", c, q) for c in range(4) for q in range(8)], 4 * 4096, BF16)
        tr.barrier()
        ALo.reset(0)
        T2 = ALo.alloc(64 * 512, BF16)
        M2 = ALo.alloc(4096, BF16)
        T2_3 = v3(T2, 512)
        M2_3 = v3(M2, 64)
        tr.op("sp", lambda e: e.dma_start(out=M2, in_=dr["M2"]), w=["M2"], dma="all2")
        T1v = T1s.rearrange("j (r k) c -> j r (k c)", r=2)
        for hh in range(2):
            for ri in range(2):
                tr.op("sp" if ri == 0 else "act", (lambda e, ri=ri, hh=hh: e.dma_start(
                    out=T2[ri * 64:(ri + 1) * 64, hh * 16384:(hh + 1) * 16384],
                    in_=T1v[:, ri, hh * 16384:(hh + 1) * 16384])),
                    r=[("T1s", j) for j in range(NT)], w=[("T2", ri, hh)], dma="t2l%d%d" % (ri, hh))
        zT4 = zT.rearrange("p (g j q) -> p g j q", g=4, j=32)
        n2 = 0
        for kg in range(8):
            for g in range(4):
                psb = n2 % 2
                n2 += 1

                def s2mm(e, g=g, kg=kg, psb=psb):
                    last = None
                    for q in range(8):
                        kk = kg * 8 + q
                        last = e.matmul(out=bank(psb)[:, q * 64:(q + 1) * 64],
                                        lhsT=T2_3[:, kk, g * 128:(g + 1) * 128], rhs=M2_3[:, kk, :],
                                        start=True, stop=True)
                    return last
                tr.op("pe", s2mm, r=["M2", ("T2", 0, kg // 4), ("T2", 1, kg // 4)], w=[pb(psb)])
                plo = kg // 4
                jj0 = (kg % 4) * 8
                eng = "dve" if n2 % 2 == 0 else "act"
                dst = zT4[:, g, jj0:jj0 + 8, plo::2]
                src = v3(bank(psb), 64)
                if eng == "dve":
                    tr.op("dve", (lambda e, dst=dst, src=src: e.tensor_copy(out=dst, in_=src)),
                          w=[pb(psb), ("zT", g, kg)])
                else:
                    tr.op("act", (lambda e, dst=dst, src=src: e.activation(out=dst, in_=src, func=AF.Copy)),
                          w=[pb(psb), ("zT", g, kg)])
        dbg_out("zT", zT, [("zT", g, kg) for g in range(4) for kg in range(8)], 4 * 4096, BF16)
        dbg_out("T2", T2, [("T2", a, b_) for a in range(2) for b_ in range(2)], 64 * 512, BF16)
        tr.barrier()
        AHi.reset(mHiC)
        ALo.reset(0)
        wO = ALo.alloc(8 * 1024, BF16)
        wO3 = v3(wO, 1024)
        wost = [ALo.alloc(1024) for _ in range(1)]
        xtc = [ALo.alloc(1024) for _ in range(2)]
        x1s = [[ALo.alloc(1024) for _ in range(4)] for _ in range(2)]
        xsc = [ALo.alloc(1024, BF16) for _ in range(4)]
        h2T = ALo.alloc(8 * 512, BF16)
        h2T3 = v3(h2T, 512)
        ot = [ALo.alloc(1024) for _ in range(1)]
        aT = AHi.alloc(NJ * 512, BF16)
        aT3 = v3(aT, 512)
        sg = [AHi.alloc(512, BF16) for _ in range(2)]
        NWR = 3
        wgr = [AHi.alloc(1024, BF16) for _ in range(NWR)]
        wur = [AHi.alloc(1024, BF16) for _ in range(NWR)]
        NDR = 4
        wdr = [AHi.alloc(512, BF16) for _ in range(NDR)]
        w_out_v = dr["w_out"].rearrange("(k p) n -> k p n", p=128)
        for k in range(8):
            s = 0
            tr.op("sp", (lambda e, s=s, k=k: e.dma_start(out=wost[s], in_=w_out_v[k])),
                  w=[("wost", s)], dma="wost%d" % s)
            tr.op("dve", (lambda e, s=s, k=k: e.tensor_tensor(out=wO3[:, k, :], in0=wost[s],
                                                              in1=gt_bc[:, 0:1024], op=ALU.mult)),
                  r=[("wost", s)] + [("gt_bc", 0, q) for q in range(4)], w=[("wO", k)])
        WOR = [("wO", k) for k in range(8)]
        zT3 = v3(zT, 4096)
        ZR = [("zT", g, kg) for g in range(4) for kg in range(8)]
        cst = {"tcn": 0, "gun": 0, "ddn": 0}

        def prologue1(qb):
            x1 = x1s[qb % 2]
            for i in range(4):
                jj = qb * 4 + i
                col0 = jj * 128
                tcn = cst["tcn"]
                xsl = tcn % 2
                tr.op("pool", (lambda e, xsl=xsl, jj=jj: e.dma_start(out=xtc[xsl], in_=dr["x"][jj])),
                      w=[("xtc", xsl)], dma="xtc%d" % xsl)
                for hh in range(2):
                    psb = hh

                    def omm(e, col0=col0, hh=hh, psb=psb):
                        last = None
                        for k in range(8):
                            lhs = zT3[:, k, col0:col0 + 128] if k < 4 else attn3[:, k - 4, col0:col0 + 128]
                            last = e.matmul(out=bank(psb), lhsT=lhs, rhs=wO3[:, k, hh * 512:(hh + 1) * 512],
                                            start=(k == 0), stop=(k == 7))
                        return last
                    tr.op("pe", omm, r=WOR + ZR + [("attnT", c, qb) for c in range(4)], w=[pb(psb)])
                    tr.op("dve", (lambda e, i=i, hh=hh, psb=psb, xsl=xsl, x1=x1: e.tensor_tensor(
                        out=x1[i][:, hh * 512:(hh + 1) * 512], in0=bank(psb),
                        in1=xtc[xsl][:, hh * 512:(hh + 1) * 512], op=ALU.add)),
                        r=[("xtc", xsl)], w=[pb(psb), ("x1", qb % 2, i, hh)])
                cst["tcn"] += 1
                rmsnorm_tile(x1[i], [("x1", qb % 2, i, 0), ("x1", qb % 2, i, 1)], xsc[i], ("xsc", i), tcn)

        def prologue2(qb):
            x1 = x1s[qb % 2]
            for i in range(4):
                tcn = qb * 4 + i
                transpose_tile(xsc[i], ("xsc", i), h2T3, ("h2T", i), i * 128, 2 + tcn % 2)

        HR = [("h2T", i) for i in range(4)]
        AR = [("aT", j) for j in range(NJ)]

        def gateup(qb):
            for j in range(NJ):
                gun = cst["gun"]
                ws = gun % NWR
                g2 = gun % 2
                cst["gun"] += 1
                tr.op("sp", (lambda e, ws=ws, j=j: e.dma_start(out=wgr[ws], in_=WGs[j])),
                      r=[("gs", j)], w=[("wgr", ws)], dma="wgr%d" % ws)
                tr.op("sp", (lambda e, ws=ws, j=j: e.dma_start(out=wur[ws], in_=WUs[j])),
                      r=[("us", j)], w=[("wur", ws)], dma="wur%d" % ws)

                def gmm(e, ws=ws, g2=g2):
                    last = None
                    for k in range(8):
                        last = e.matmul(out=bank(4 + g2), lhsT=wgr[ws][:, k * 128:(k + 1) * 128],
                                        rhs=h2T3[:, k, :], start=(k == 0), stop=(k == 7))
                    return last

                def umm(e, ws=ws, g2=g2):
                    last = None
                    for k in range(8):
                        last = e.matmul(out=bank(6 + g2), lhsT=wur[ws][:, k * 128:(k + 1) * 128],
                                        rhs=h2T3[:, k, :], start=(k == 0), stop=(k == 7))
                    return last
                tr.op("pe", gmm, r=HR + [("wgr", ws)], w=[pb(4 + g2)])
                tr.op("pe", umm, r=HR + [("wur", ws)], w=[pb(6 + g2)])
                tr.op("act", (lambda e, g2=g2, j=j: e.activation(out=sg[g2], in_=bank(4 + g2), func=AF.Silu,
                                                                 bias=bgu[:, j:j + 1], scale=1.0)),
                      r=[("bgu", j)], w=[pb(4 + g2), ("sg", g2)])
                tr.op("dve", (lambda e, g2=g2, j=j: e.scalar_tensor_tensor(
                    out=aT3[:, j, :], in0=bank(6 + g2), scalar=bgu[:, NJ + j:NJ + j + 1], in1=sg[g2],
                    op0=ALU.add, op1=ALU.mult)), r=[("bgu", NJ + j), ("sg", g2)], w=[pb(6 + g2), ("aT", j)])

        def down(qb, hh):
            x1 = x1s[qb % 2]
            for j in range(NJ):
                ddn = cst["ddn"]
                ds_ = ddn % NDR
                cst["ddn"] += 1
                tr.op("sp", (lambda e, ds_=ds_, j=j, hh=hh: e.dma_start(
                    out=wdr[ds_], in_=WDs[j][:, hh * 512:(hh + 1) * 512])),
                    r=[("ds", j)], w=[("wdr", ds_)], dma="wdr%d" % ds_)

                def dmm(e, ds_=ds_, j=j):
                    last = None
                    for i in range(4):
                        last = e.matmul(out=bank(4 + i), lhsT=aT3[:, j, i * 128:(i + 1) * 128],
                                        rhs=wdr[ds_], start=(j == 0), stop=(j == NJ - 1))
                    return last
                tr.op("pe", dmm, r=AR + [("wdr", ds_)], w=[pb(4 + i) for i in range(4)])
            for i in range(4):
                tr.op("dve", (lambda e, i=i, hh=hh, x1=x1: e.tensor_tensor(
                    out=x1[i][:, hh * 512:(hh + 1) * 512], in0=bank(4 + i),
                    in1=x1[i][:, hh * 512:(hh + 1) * 512], op=ALU.add)),
                    w=[pb(4 + i), ("x1", qb % 2, i, hh)])

        def final(qb):
            x1 = x1s[qb % 2]
            for i in range(4):
                jj = qb * 4 + i
                os_ = 0
                s4 = jj % 4
                XRES = [("x1", qb % 2, i, 0), ("x1", qb % 2, i, 1)]
                tr.op("act", (lambda e, i=i, s4=s4, x1=x1: e.activation(out=junk, in_=x1[i], func=AF.Square,
                                                                       accum_out=ssb[s4])),
                      r=XRES, w=["junk", ("ss", s4)])
                tr.op("act", (lambda e, s4=s4: e.activation(out=tmb[s4], in_=ssb[s4], func=AF.Ln, bias=eps_c,
                                                            scale=1.0 / D)),
                      r=[("ss", s4), "eps_c"], w=[("tm", s4)])
                tr.op("act", (lambda e, s4=s4: e.activation(out=rstd[s4], in_=tmb[s4], func=AF.Exp, scale=-0.5)),
                      r=[("tm", s4)], w=[("rstd", s4)])
                tr.op("dve", (lambda e, i=i, s4=s4, os_=os_, x1=x1: e.scalar_tensor_tensor(
                    out=ot[os_], in0=x1[i], scalar=rstd[s4], in1=gfin, op0=ALU.mult, op1=ALU.mult)),
                    r=[("rstd", s4), "gfin"] + XRES, w=[("ot", os_)])
                tr.op("act", (lambda e, os_=os_, jj=jj: e.dma_start(out=out_d[jj], in_=ot[os_])),
                      r=[("ot", os_)], w=[("out", jj)], dma="ost%d" % os_)

        PIPE_C = True
        if not PIPE_C:
            for qb in range(8):
                prologue1(qb)
                prologue2(qb)
                gateup(qb)
                down(qb, 0)
                down(qb, 1)
                final(qb)
        else:
            prologue1(0)
            prologue2(0)
            for qb in range(8):
                gateup(qb)
                if qb + 1 < 8:
                    prologue1(qb + 1)
                    prologue2(qb + 1)
                down(qb, 0)
                down(qb, 1)
                final(qb)
        tr.op("sp", lambda e: None, r=[("out", jj) for jj in range(NOWN)] + [k for k in tr.lastw if isinstance(k, tuple) and k[0] == "dbg"], w=["done"])

        semnames = tr.resolve()
        sems = {n: st.enter_context(nc.semaphore(n)) for n in semnames}
        block = st.enter_context(nc.Block())

        @block.tensor
        def _(e):
            tr.emit("pe", e, sems)

        @block.scalar
        def _(e):
            tr.emit("act", e, sems)

        @block.vector
        def _(e):
            tr.emit("dve", e, sems)

        @block.gpsimd
        def _(e):
            tr.emit("pool", e, sems)

        @block.sync
        def _(e):
            tr.emit("sp", e, sems)
    return nc


_CACHE = {}


def _col8(v):
    return np.ascontiguousarray(np.asarray(v, np.float32).reshape(8, 128).T)


def make_in_maps(x, c, ctx, c_ctx, w_ada, b_ada, g_mix, w_in, w_four, q_gain, k_gain,
                 w_out, g_ffn, w_gate, w_up, w_down, g_final):
    f = lambda a: np.asarray(a, np.float32)
    x, c, ctx, c_ctx = f(x), f(c), f(ctx), f(c_ctx)
    w_ada, b_ada, g_mix, w_in = f(w_ada)[0], f(b_ada)[0], f(g_mix)[0], f(w_in)[0]
    w_four, q_gain, k_gain, w_out = f(w_four)[0], f(q_gain)[0], f(k_gain)[0], f(w_out)[0]
    g_ffn, w_gate, w_up, w_down, g_final = f(g_ffn)[0], f(w_gate)[0], f(w_up)[0], f(w_down)[0], f(g_final)
    qcols = []
    for cpair in range(4):
        qcols += list(range(512 + cpair * 64, 512 + cpair * 64 + 64))
        qcols += list(range(512 + (4 + cpair) * 64, 512 + (4 + cpair) * 64 + 64))
    cols = list(range(512)) + qcols + list(range(1024, 1280))
    w_in_r = np.ascontiguousarray(w_in[:, cols])
    rows = list(range(512))
    for cpair in range(4):
        rows += list(range(512 + cpair * 64, 512 + cpair * 64 + 64))
        rows += list(range(512 + (4 + cpair) * 64, 512 + (4 + cpair) * 64 + 64))
    w_out_r = np.ascontiguousarray(w_out[rows, :])
    w4 = np.ascontiguousarray(w_four.transpose(1, 0, 2).reshape(128, 512))
    badac = np.ascontiguousarray(b_ada.reshape(48, 128).T)
    badag = np.ascontiguousarray(np.broadcast_to(
        np.concatenate([b_ada[2048:3072], b_ada[5120:6144]])[None, :], (128, 2048)))
    gvec = np.concatenate([_col8(g_mix), _col8(g_ffn)], 1)
    gfin = np.ascontiguousarray(np.broadcast_to(g_final[None, :], (128, D)))
    qkg = np.stack([np.tile(q_gain, 2), np.tile(k_gain, 2)], 1).astype(np.float32)
    consts = {hf: host_consts(hf) for hf in range(2)}
    maps = []
    for core in range(8):
        b, hf = core // 2, core % 2
        order = tile_order(hf)
        xg = x[b].reshape(128, NT, D).transpose(1, 0, 2)[order]
        cc = np.zeros((128, 16), np.float32)
        cc[:, 0::2] = _col8(c[b])
        cc[:, 1::2] = _col8(c_ctx)
        m = {
            "x": np.ascontiguousarray(xg), "ctx": np.ascontiguousarray(ctx[b].reshape(2, 128, D)),
            "cc": cc, "w_ada": w_ada, "badac": badac, "badag": badag, "gvec": gvec, "gfin": gfin,
            "qkg": qkg, "w_in": w_in_r, "w_four": w4, "w_out": w_out_r,
            "w_gate": w_gate, "w_up": w_up, "w_down": w_down,
        }
        m.update(consts[hf])
        maps.append(m)
    return maps


def kernel(**inputs):
    if "nc" not in _CACHE:
        _CACHE["nc"] = build_program()
    nc = _CACHE["nc"]
    maps = make_in_maps(**inputs)
    res = run_bass_kernel_spmd(nc, maps, core_ids=list(range(8)))
    _CACHE["res"] = res
    out = np.zeros((B, S, D), np.float32)
    for core in range(8):
        b, hf = core // 2, core % 2
        o = np.asarray(res.results[core]["out"], np.float32)
        ov = out[b].reshape(128, NT, D)
        for jj in range(NOWN):
            ov[:, 32 * hf + jj, :] = o[jj]
    return out
```
